# Optimizing a Trainium2 kernel written in Bass

```python
import jax, jax.numpy as jnp
from jax import lax
import numpy as np

D_MODEL = 1024
BATCH = 16
SEQ = 256
DEPTH = 4
DEC_BATCH = 2
DEC_SEQ = 2048
PAST_LEN = 512

GRID_W = 64
N_MIXERS = 2
N_GLA_LAYERS = (DEPTH + N_MIXERS - 1) // N_MIXERS
N_FNET_LAYERS = DEPTH // N_MIXERS
GLA_HEADS = 4
GLA_QK = D_MODEL // 2
GLA_V = D_MODEL
GLA_DK = GLA_QK // GLA_HEADS
GLA_DV = GLA_V // GLA_HEADS
GATE_RANK = 16
GATE_TAU = 16.0
CHUNK = 64
GLA_IN = 2 * GLA_QK + 2 * GLA_V + 2 * GATE_RANK
FNET_GROUPS = 4
FNET_GW = D_MODEL // FNET_GROUPS
FFN_DIM = 2816
CONV_W = 3
EPS = 1e-6

kernel_name = "hybrid_gla_fnet_diffusion_step"


def rmsnorm(x, g):
    xf = x.astype(jnp.float32)
    y = xf * lax.rsqrt(jnp.mean(xf * xf, axis=-1, keepdims=True) + EPS)
    return (y * g.astype(jnp.float32)).astype(x.dtype)


def ada_mod(cvec, w, b):
    m = jax.nn.silu(cvec) @ w + b
    return jnp.split(m[:, None, :], 6, axis=-1)


def gla_chunked(q, k, v, logg, s0):
    B, L, H, DK = q.shape
    DV = v.shape[-1]
    nc = L // CHUNK

    def to_chunks(t):
        return t.astype(jnp.float32).reshape(B, nc, CHUNK, H, t.shape[-1]).transpose(1, 0, 3, 2, 4)

    qc, kc, vc, gc = to_chunks(q), to_chunks(k), to_chunks(v), to_chunks(logg)
    lower = jnp.tril(jnp.ones((CHUNK, CHUNK), dtype=bool))

    def step(state, inp):
        qi, ki, vi, gi = inp
        bcum = jnp.cumsum(gi, axis=2)
        diff = bcum[:, :, :, None, :] - bcum[:, :, None, :, :]
        decay = jnp.exp(jnp.where(lower[:, :, None], diff, -jnp.inf))
        attn = jnp.einsum('bhik,bhjk,bhijk->bhij', qi, ki, decay)
        o = (jnp.einsum('bhij,bhjv->bhiv', attn, vi)
             + jnp.einsum('bhik,bhkv->bhiv', qi * jnp.exp(bcum), state))
        b_last = bcum[:, :, -1:, :]
        state = (jnp.exp(b_last[:, :, 0, :])[..., None] * state
                 + jnp.einsum('bhjk,bhjv->bhkv', ki * jnp.exp(b_last - bcum), vi))
        return state, o

    s_fin, o = lax.scan(step, s0.astype(jnp.float32), (qc, kc, vc, gc))
    o = o.transpose(1, 0, 3, 2, 4).reshape(B, L, H, DV)
    return o, s_fin


def gla_mixer(h, w_in, w_g2, b_g, norm_g, w_o, s0_f, s0_b):
    B, L, _ = h.shape
    p = h @ w_in
    q, k, v, r, glr = jnp.split(p, [GLA_QK, 2 * GLA_QK, 2 * GLA_QK + GLA_V, 2 * GLA_QK + 2 * GLA_V], axis=-1)
    q = q.reshape(B, L, GLA_HEADS, GLA_DK) * (GLA_DK ** -0.5)
    k = k.reshape(B, L, GLA_HEADS, GLA_DK)
    v = v.reshape(B, L, GLA_HEADS, GLA_DV)
    glr = glr.reshape(B, L, 2, GATE_RANK)
    logit = jnp.einsum('blzr,zrk->blzk', glr, w_g2) + b_g
    logg = (jax.nn.log_sigmoid(logit.astype(jnp.float32)) / GATE_TAU).reshape(B, L, 2, GLA_HEADS, GLA_DK)
    o_f, s_f = gla_chunked(q, k, v, logg[:, :, 0], s0_f)
    o_b, s_b = gla_chunked(jnp.flip(q, 1), jnp.flip(k, 1), jnp.flip(v, 1), jnp.flip(logg[:, :, 1], 1), s0_b)
    o = (o_f + jnp.flip(o_b, 1)).astype(h.dtype)
    o = rmsnorm(o, norm_g) * jax.nn.silu(r.reshape(B, L, GLA_HEADS, GLA_DV))
    return o.reshape(B, L, GLA_V) @ w_o, s_f, s_b


def fourier_mixer(h, w_in, w_o):
    B, L, _ = h.shape
    u = (h @ w_in).reshape(B, L, FNET_GROUPS, FNET_GW).astype(jnp.float32)
    f = jnp.fft.fftn(u, axes=(1, 3), norm='ortho').real
    return f.astype(h.dtype).reshape(B, L, D_MODEL) @ w_o


def dwconv3(a, w, b):
    ap = jnp.pad(a, [(0, 0)] * (a.ndim - 2) + [(1, 1), (0, 0)])
    return ap[..., :-2, :] * w[0] + ap[..., 1:-1, :] * w[1] + ap[..., 2:, :] * w[2] + b


def conv_ffn(h, w_up, conv_w, conv_b, w_down, rows):
    B, L, _ = h.shape
    a, g = jnp.split(h @ w_up, 2, axis=-1)
    if rows is None:
        a = dwconv3(a, conv_w, conv_b)
    else:
        a = dwconv3(a.reshape(B, rows, GRID_W, FFN_DIM), conv_w, conv_b).reshape(B, L, FFN_DIM)
    return (jax.nn.silu(a) * g) @ w_down


def setup_inputs(seed: int = 0) -> dict:
    key = jax.random.key(seed)
    ks = jax.random.split(key, 24)
    f32 = jnp.float32
    nrm = lambda k, shape, s: jax.random.normal(k, shape, f32) * s
    D = D_MODEL
    return {
        'x_prompt': nrm(ks[0], (BATCH, SEQ, D), 1.0),
        'x_sample': nrm(ks[1], (DEC_BATCH, DEC_SEQ, D), 1.0),
        'state_gla': nrm(ks[2], (DEC_BATCH, N_GLA_LAYERS, 2, GLA_HEADS, GLA_DK, GLA_DV), 0.5),
        'c': nrm(ks[3], (DEC_BATCH, D), 1.0),
        'c_ctx': nrm(ks[4], (D,), 1.0),
        'norm_mix_g': 1.0 + nrm(ks[5], (DEPTH, D), 0.02),
        'norm_ffn_g': 1.0 + nrm(ks[6], (DEPTH, D), 0.02),
        'w_mod': nrm(ks[7], (DEPTH, D, 6 * D), 0.5 * D ** -0.5),
        'b_mod': nrm(ks[8], (DEPTH, 6 * D), 0.02),
        'gla_w_in': nrm(ks[9], (N_GLA_LAYERS, D, GLA_IN), D ** -0.5),
        'gla_w_g2': nrm(ks[10], (N_GLA_LAYERS, 2, GATE_RANK, GLA_QK), GATE_RANK ** -0.5),
        'gla_b_g': nrm(ks[11], (N_GLA_LAYERS, 2, GLA_QK), 0.1),
        'gla_norm_g': 1.0 + nrm(ks[12], (N_GLA_LAYERS, GLA_DV), 0.02),
        'gla_w_o': nrm(ks[13], (N_GLA_LAYERS, GLA_V, D), GLA_V ** -0.5),
        'fnet_w_in': nrm(ks[14], (N_FNET_LAYERS, D, D), D ** -0.5),
        'fnet_w_o': nrm(ks[15], (N_FNET_LAYERS, D, D), D ** -0.5),
        'ffn_w_up': nrm(ks[16], (DEPTH, D, 2 * FFN_DIM), D ** -0.5),
        'ffn_conv_w': nrm(ks[17], (DEPTH, CONV_W, FFN_DIM), CONV_W ** -0.5),
        'ffn_conv_b': nrm(ks[18], (DEPTH, FFN_DIM), 0.02),
        'ffn_w_down': nrm(ks[19], (DEPTH, FFN_DIM, D), FFN_DIM ** -0.5),
        'final_norm_g': 1.0 + nrm(ks[20], (D,), 0.02),
    }


def reference(x_prompt, x_sample, state_gla, c, c_ctx, norm_mix_g, norm_ffn_g, w_mod, b_mod,
              gla_w_in, gla_w_g2, gla_b_g, gla_norm_g, gla_w_o, fnet_w_in, fnet_w_o,
              ffn_w_up, ffn_conv_w, ffn_conv_b, ffn_w_down, final_norm_g):
    ctx = x_prompt
    lat = x_sample
    bsz = ctx.shape[0]
    rows = lat.shape[1] // GRID_W
    zero_state = jnp.zeros((bsz, GLA_HEADS, GLA_DK, GLA_DV), jnp.float32)
    new_states = []
    for i in range(DEPTH):
        sh1c, sc1c, gt1c, sh2c, sc2c, gt2c = ada_mod(c_ctx[None, :], w_mod[i], b_mod[i])
        sh1l, sc1l, gt1l, sh2l, sc2l, gt2l = ada_mod(c, w_mod[i], b_mod[i])
        hc = rmsnorm(ctx, norm_mix_g[i]) * (1 + sc1c) + sh1c
        hl = rmsnorm(lat, norm_mix_g[i]) * (1 + sc1l) + sh1l
        if i % N_MIXERS == 0:
            j = i // N_MIXERS
            yc, s_f, s_b = gla_mixer(hc, gla_w_in[j], gla_w_g2[j], gla_b_g[j], gla_norm_g[j], gla_w_o[j],
                                     zero_state, zero_state)
            new_states.append(jnp.stack([s_f, s_b], axis=1).astype(ctx.dtype))
            yl, _, _ = gla_mixer(hl, gla_w_in[j], gla_w_g2[j], gla_b_g[j], gla_norm_g[j], gla_w_o[j],
                                 state_gla[:, j, 0], state_gla[:, j, 1])
        else:
            j = i // N_MIXERS
            yc = fourier_mixer(hc, fnet_w_in[j], fnet_w_o[j])
            yl = fourier_mixer(hl, fnet_w_in[j], fnet_w_o[j])
        ctx = ctx + gt1c * yc
        lat = lat + gt1l * yl
        hc = rmsnorm(ctx, norm_ffn_g[i]) * (1 + sc2c) + sh2c
        hl = rmsnorm(lat, norm_ffn_g[i]) * (1 + sc2l) + sh2l
        ctx = ctx + gt2c * conv_ffn(hc, ffn_w_up[i], ffn_conv_w[i], ffn_conv_b[i], ffn_w_down[i], None)
        lat = lat + gt2l * conv_ffn(hl, ffn_w_up[i], ffn_conv_w[i], ffn_conv_b[i], ffn_w_down[i], rows)
    y_prompt = rmsnorm(ctx, final_norm_g)
    y_sample = rmsnorm(lat, final_norm_g)
    new_state_gla = jnp.stack(new_states, axis=1)
    return (y_prompt, y_sample, new_state_gla)
```

```python
import os
import numpy as np
import ml_dtypes
from contextlib import ExitStack
import concourse.bass as bass
import concourse.mybir as mybir
from concourse.bass_utils import run_bass_kernel_spmd

F32 = mybir.dt.float32
BF16 = mybir.dt.bfloat16
AF = mybir.ActivationFunctionType
ALU = mybir.AluOpType

D = 1024
DEPTH = 4
FFN = 2816
NJ = 22
EPS = 1e-6
NT = 1024
ENGS = ['pe', 'act', 'dve', 'pool', 'sp']
CH = 2000
FLAGS = set(os.environ.get("KFLAGS", "gla,fnet,ffn").split(","))
NLAYERS = int(os.environ.get("KLAYERS", "4"))


class Ins:
    __slots__ = ('eng', 'fn', 'deps', 'sig', 'kind', 'inc', 'ms', 'sem', 'val', 'n', 'prevwait')


class Sched:
    def __init__(self):
        self.L = {e: [] for e in ENGS}
        self.recs = {}

    def add(self, eng, fn, reads=(), writes=(), kind='c', inc=1):
        lst = self.L[eng]
        idx = len(lst)
        ins = Ins()
        ins.eng = eng; ins.fn = fn; ins.kind = kind; ins.inc = inc
        ins.sig = (kind != 'c')
        deps = set()
        inorder = (kind == 'c')
        psr = [(sp, lo // 2048 * 2048, (hi + 2047) // 2048 * 2048) for (sp, lo, hi) in reads if sp == 'ps']
        psw = [(sp, lo // 2048 * 2048, (hi + 2047) // 2048 * 2048) for (sp, lo, hi) in writes if sp == 'ps']
        reads = [r for r in reads if r[0] != 'ps']
        writes = [w for w in writes if w[0] != 'ps'] + psr + psw
        for (sp, lo, hi) in reads:
            rl = self.recs.setdefault(sp, [])
            for r in rl:
                if r[2] == 'W' and r[0] < hi and lo < r[1]:
                    deps.add((r[3], r[4]))
        for (sp, lo, hi) in writes:
            rl = self.recs.setdefault(sp, [])
            keep = []
            for r in rl:
                if r[0] < hi and lo < r[1]:
                    deps.add((r[3], r[4]))
                    if r[0] >= lo and r[1] <= hi:
                        continue
                keep.append(r)
            keep.append((lo, hi, 'W', eng, idx))
            self.recs[sp] = keep
        for (sp, lo, hi) in reads:
            rl = self.recs[sp]
            if inorder:
                rl = [r for r in rl if not (r[2] == 'R' and r[3] == eng and r[0] >= lo and r[1] <= hi)]
            rl.append((lo, hi, 'R', eng, idx))
            self.recs[sp] = rl
        if eng == 'pe':
            deps = {d for d in deps if d[0] != 'pe'}
        deps.discard((eng, idx))
        ins.deps = deps
        lst.append(ins)
        return ins

    def finalize(self, nc, stack):
        for e in ENGS:
            for ins in self.L[e]:
                for (de, di) in ins.deps:
                    self.L[de][di].sig = True
        self.bank = {}
        self.pool = {}
        NPOOL = 8
        for e in ENGS:
            ms = 0
            na = 0
            for ins in self.L[e]:
                if ins.kind == 'c':
                    if ins.sig:
                        ins.ms = ms
                        ms += 1
                elif ins.kind == 'd':
                    ins.n = na
                    na += 1
            nb = (ms + CH - 1) // CH
            self.bank[e] = [stack.enter_context(nc.semaphore("b_%s_%d" % (e, i))) for i in range(nb)]
            if na:
                self.pool[e] = [stack.enter_context(nc.semaphore("a_%s_%d" % (e, i))) for i in range(min(NPOOL, na))]
            ncc = 0
            for ins in self.L[e]:
                if ins.kind == 'c':
                    if ins.sig:
                        ins.sem = self.bank[e][ins.ms // CH]
                        ins.val = ins.ms % CH + 1
                elif ins.kind == 'cc':
                    ins.sem = stack.enter_context(nc.semaphore("cc_%s_%d" % (e, ncc)))
                    ncc += 1
                else:
                    P = len(self.pool[e])
                    ins.sem = self.pool[e][ins.n % P]
                    ins.val = ins.inc * 0
        for e in ENGS:
            if e not in self.pool:
                continue
            cum = {}
            prev = {}
            lastcc = None
            for ins in self.L[e]:
                if ins.kind == 'cc':
                    ins.prevwait = lastcc
                    ins.val = 1
                    lastcc = (ins.sem, 1)
                    continue
                if ins.kind != 'c':
                    k = id(ins.sem)
                    ins.prevwait = prev.get(k)
                    cum[k] = cum.get(k, 0) + ins.inc
                    ins.val = cum[k]
                    prev[k] = (ins.sem, ins.val)

    def emit(self, eng, engobj):
        waited_ms = {e: -1 for e in ENGS}
        waited_async = {}
        for ins in self.L[eng]:
            waits = []
            if ins.kind != 'c' and getattr(ins, 'prevwait', None) is not None:
                waits.append(('a', ins.prevwait[0], ins.prevwait[1]))
            for (de, di) in sorted(ins.deps, key=lambda t: (t[0], t[1])):
                d = self.L[de][di]
                if d.kind == 'c':
                    if d.ms > waited_ms[de]:
                        waits.append(('c', de, d.ms))
                else:
                    waits.append(('a', d.sem, d.val))
            best = {}
            for w in waits:
                if w[0] == 'c':
                    best[w[1]] = max(best.get(w[1], -1), w[2])
            for de, m in best.items():
                engobj.wait_ge(self.bank[de][m // CH], m % CH + 1)
                waited_ms[de] = m
            for w in waits:
                if w[0] == 'a':
                    k = id(w[1])
                    if waited_async.get(k, 0) < w[2]:
                        engobj.wait_ge(w[1], w[2])
                        waited_async[k] = w[2]
            if ins.fn is not None:
                h = ins.fn(engobj)
                if ins.sig:
                    h.then_inc(ins.sem, ins.inc)


class View:
    def __init__(self, arena, off, shape, dt):
        self.off = off
        self.shape = tuple(shape)
        self.dt = dt
        self.esz = 2 if dt == BF16 else 4
        n = int(np.prod(shape))
        self.n = n
        ap = arena[:, off // 2: off // 2 + n * self.esz // 2]
        if dt != BF16:
            ap = ap.bitcast(dt)
        if len(shape) == 2:
            ap = ap.rearrange('p (a b) -> p a b', a=shape[0])
        elif len(shape) == 3:
            ap = ap.rearrange('p (a b c) -> p a b c', a=shape[0], b=shape[1])
        self.ap = ap
        st = []
        s = 1
        for d_ in reversed(self.shape):
            st.append(s)
            s *= d_
        self.strides = tuple(reversed(st))

    def __getitem__(self, idx):
        return self.ap[idx]

    def reg(self, *idx):
        lo = 0
        hi = 0
        for i, d_ in enumerate(self.shape):
            if i < len(idx) and idx[i] is not None:
                it = idx[i]
                if isinstance(it, int):
                    a, b = it, it + 1
                else:
                    a, b = it
            else:
                a, b = 0, d_
            lo += a * self.strides[i]
            hi += (b - 1) * self.strides[i]
        hi += 1
        return ('sb', self.off + lo * self.esz, self.off + hi * self.esz)


class Arena:
    def __init__(self, ap, nbytes):
        self.ap = ap
        self.nbytes = nbytes
        self.top = 0

    def alloc(self, shape, dt):
        esz = 2 if dt == BF16 else 4
        n = int(np.prod(shape)) * esz
        off = (self.top + 31) // 32 * 32
        assert off + n <= self.nbytes, "arena overflow %d > %d" % (off + n, self.nbytes)
        self.top = off + n
        self.peak = max(getattr(self, 'peak', 0), self.top)
        return View(self.ap, off, shape, dt)

    def mark(self):
        return self.top

    def release(self, m):
        self.top = m


ARENA_BYTES = 206 * 1024
NSLOT = 12


def build_program():
    nc = bass.Bass("TRN2", target_bir_lowering=False)
    S = Sched()

    def din(name, shape, dt=F32):
        return nc.dram_tensor(name, list(shape), dt, kind="ExternalInput")

    d_xT = din("xT", [128, 8 * NT])
    d_cT = din("cT", [128, 24])
    d_wmod = din("wmodsh", [DEPTH * 12, 128, 1024])
    d_bmod = din("bmodsh", [128, 144])
    d_g2 = din("g2", [128, 9 * 16])
    d_wup = din("wup", [DEPTH * NJ, 128, 2048])
    d_wdn = din("wdn", [DEPTH * NJ, 128, 1024])
    d_cw = din("cw", [128, DEPTH * NJ * 4])
    d_fwin = din("fwin", [2 * 8, 128, 1024])
    d_fwo = din("fwo", [2 * 8, 128, 1024])
    d_gwo = din("gwo", [2 * 8, 128, 1024])
    d_gwin = din("gwin", [2 * 4, 128, 6400])
    d_gwins = din("gwins", [2, 128, 6400])
    d_wg2 = din("wg2", [2 * 4, 33, 256])
    d_wg2s = din("wg2s", [2, 33, 256])
    d_gng = din("gng", [128, 4])
    d_st0 = din("st0", [4, 128, 256])
    d_cf32 = din("cf32", [128, 4 * 128 + 4])
    d_cbf = din("cbf", [128, 3 * 128 + 4 * 512], BF16)
    d_posL = din("posL", [128, 2 * 16 * 512], BF16)
    d_sel = din("sel", [128, 14])
    d_yT = nc.dram_tensor("yT", [128, 8 * NT], F32, kind="ExternalOutput")
    d_ns = nc.dram_tensor("ns", [2 * 2 * 2 * 4, 128, 256], F32, kind="ExternalOutput")
    ag_h_in = nc.dram_tensor("ag_h_in", [1024, 512], BF16)
    ag_h_out = nc.dram_tensor("ag_h_out", [4096, 512], BF16)
    ag_o_in = nc.dram_tensor("ag_o_in", [256, 2048], BF16)
    ag_o_out = nc.dram_tensor("ag_o_out", [1024, 2048], BF16)
    ag_u_in = [nc.dram_tensor("ag_u%d_in" % i, [512, 1024], BF16) for i in range(2)]
    ag_u_out = [nc.dram_tensor("ag_u%d_out" % i, [2048, 1024], BF16) for i in range(2)]
    ag_x_in = nc.dram_tensor("ag_x_in", [1024, 128], BF16)
    ag_x_out = nc.dram_tensor("ag_x_out", [4096, 128], BF16)
    ag_m_in = [nc.dram_tensor("ag_m0_in", [128, 36], F32), nc.dram_tensor("ag_m1_in", [128, 108], F32)]
    ag_m_out = [nc.dram_tensor("ag_m0_out", [512, 36], F32), nc.dram_tensor("ag_m1_out", [512, 108], F32)]
    RG = [[0, 1, 2, 3], [4, 5, 6, 7]]

    stack = ExitStack()
    arena_t = stack.enter_context(nc.sbuf_tensor("arena", [128, ARENA_BYTES // 2], BF16))
    ps_t = stack.enter_context(nc.psum_tensor("ps", [128, 4096], F32))
    AR = Arena(arena_t, ARENA_BYTES)

    ps_state = {'p': 0}

    def psalloc(ncols):
        nrot = ps_state.get('nrot', 8)
        p = ps_state['p']
        if p >= nrot:
            p = 0
        ps_state['p'] = (p + 1) % nrot
        lo = p * 512
        return lo, ('ps', lo * 4, (lo + ncols) * 4)

    def PS(lo, ncols, parts=128):
        return ps_t[0:parts, lo:lo + ncols]

    def mm(out, oreg, lhsT, lreg, rhs, rreg, start, stop):
        S.add('pe', lambda e: e.matmul(out, lhsT, rhs, start=start, stop=stop), reads=[lreg, rreg], writes=[oreg])

    def act(out, in_, func, reads, writes, bias=None, scale=None):
        kw = {}
        if bias is not None:
            kw['bias'] = bias
        if scale is not None:
            kw['scale'] = scale
        S.add('act', lambda e: e.activation(out, in_, func, **kw), reads=reads, writes=writes)

    def ts(out, in0, s1, s2, op0, op1, reads, writes):
        if op1 is None:
            S.add('dve', lambda e: e.tensor_scalar(out, in0, s1, None, op0), reads=reads, writes=writes)
        else:
            S.add('dve', lambda e: e.tensor_scalar(out, in0, s1, s2, op0, op1), reads=reads, writes=writes)

    def stt(out, in0, sc, in1, op0, op1, reads, writes):
        S.add('dve', lambda e: e.scalar_tensor_tensor(out, in0, sc, in1, op0, op1), reads=reads, writes=writes)

    def tt(out, in0, in1, op, reads, writes):
        S.add('dve', lambda e: e.tensor_tensor(out, in0, in1, op), reads=reads, writes=writes)

    def dma(q, out, in_, reads, writes):
        S.add(q, lambda e: e.dma_start(out=out, in_=in_), reads=reads, writes=writes, kind='d', inc=16)

    def allgather(in_t, out_t, groups=None):
        groups = groups or RG
        S.add('pool', lambda e: e.collective_compute("AllGather", ALU.bypass, replica_groups=groups,
                                                      ins=[in_t.ap().opt()], outs=[out_t.ap().opt()]),
              reads=[('d:' + in_t.name, 0, 1)], writes=[('d:' + out_t.name, 0, 1)], kind='cc', inc=1)

    def dreg(t):
        return ('d:' + t.name, 0, 1)

    x = AR.alloc([8, NT], F32)
    h = AR.alloc([8, NT], BF16)
    ring = AR.alloc([NSLOT, 1024], BF16)
    cf32 = AR.alloc([4 * 128 + 4], F32)
    cbf = AR.alloc([3 * 128 + 4 * 512], BF16)
    sel = AR.alloc([14], F32)
    cT = AR.alloc([24], F32)
    scb = AR.alloc([8, 3], BF16)
    bmod = AR.alloc([144], F32)
    msh = AR.alloc([144], F32)
    mall = AR.alloc([4, 144], F32)
    g2 = AR.alloc([9 * 16], F32)
    cw = AR.alloc([DEPTH * NJ * 4], F32)
    gng = AR.alloc([4], F32)
    modv = AR.alloc([DEPTH, 96], F32)
    A1 = AR.alloc([DEPTH, 16], F32)
    A2 = AR.alloc([DEPTH, 16], F32)

    ring_state = {'p': 0}
    ns_keys = []
    y_keys = []

    def walloc(nelem):
        ns = (nelem + 1023) // 1024
        p = ring_state['p']
        if p + ns > NSLOT:
            p = 0
        ring_state['p'] = p + ns
        off = ring.off + p * 2048
        return off

    def wload(dram_ap, shape):
        n = int(np.prod(shape))
        off = walloc(n)
        v = View(arena_t, off, shape, BF16)
        dma('pool', v.ap, dram_ap, reads=[], writes=[('sb', off, off + n * 2)])
        return v

    def cf(i):
        return cf32.ap[:, i * 128:(i + 1) * 128], cf32.reg((i * 128, (i + 1) * 128))
    one_col = cf32.ap[:, 512:513]
    one_col_reg = cf32.reg((512, 516))
    eps_col = cf32.ap[:, 513:514]
    maskF = cbf.ap[:, 0:128]; maskB = cbf.ap[:, 128:256]; ones_bf = cbf.ap[:, 256:384]
    cbf_reg = cbf.reg()
    chC = cbf.ap[:, 384:896].rearrange('p (a b) -> p a b', a=2)
    chS = cbf.ap[:, 896:1408].rearrange('p (a b) -> p a b', a=2)
    pC = cbf.ap[:, 1408:1920].rearrange('p (a b) -> p a b', a=2)
    pSn = cbf.ap[:, 1920:2432].rearrange('p (a b) -> p a b', a=2)

    dma('sp', cT.ap, d_cT[:, :], [], [cT.reg()])
    dma('sp', bmod.ap, d_bmod[:, :], [], [bmod.reg()])
    dma('sp', sel.ap, d_sel[:, :], [], [sel.reg()])
    dma('sp', g2.ap, d_g2[:, :], [], [g2.reg()])
    dma('sp', cf32.ap, d_cf32[:, :], [], [cf32.reg()])
    dma('sp', cbf.ap, d_cbf[:, :], [], [cbf.reg()])
    dma('sp', cw.ap, d_cw[:, :], [], [cw.reg()])
    dma('sp', gng.ap, d_gng[:, :], [], [gng.reg()])
    for k in range(8):
        dma('sp', x[:, k, :], d_xT[:, k * NT:(k + 1) * NT], [], [x.reg(k)])
    act(scb.ap.rearrange('p a b -> p (a b)'), cT.ap, AF.Silu, [cT.reg()], [scb.reg()])

    def mod_cols(part):
        layers = [0] if part == 0 else [1, 2, 3]
        return layers, layers[0] * 36, len(layers) * 36

    def mod_block(i):
        lo, preg = psalloc(128)
        w = wload(d_wmod[i, :, :], [1024])
        for k in range(8):
            mm(PS(lo, 3), preg, w.ap[:, k * 128:(k + 1) * 128], w.reg((k * 128, (k + 1) * 128)),
               scb[:, k, :], scb.reg(k), k == 0, k == 7)
        tt(msh.ap[:, i * 3:(i + 1) * 3], PS(lo, 3), bmod.ap[:, i * 3:(i + 1) * 3], ALU.add, [preg, bmod.reg()], [msh.reg((i * 3, (i + 1) * 3))])

    def mod_launch(part):
        layers, c0, ncol = mod_cols(part)
        dma('sp', ag_m_in[part][:, :], msh.ap[:, c0:c0 + ncol], [msh.reg((c0, c0 + ncol))], [dreg(ag_m_in[part])])
        allgather(ag_m_in[part], ag_m_out[part])

    def mod_finish(part):
        layers, c0, ncol = mod_cols(part)
        dma('sp', mall.ap[:, :, c0:c0 + ncol], ag_m_out[part][:, :].rearrange('(r p) n -> p r n', p=128),
            [dreg(ag_m_out[part])], [mall.reg(None, (c0, c0 + ncol))])
        for l in layers:
            src = mall.ap[:, :, l * 36:(l + 1) * 36].rearrange('p r (c v) -> p r c v', v=3)
            dst = modv[:, l, :].rearrange('p (r c v) -> p r c v', r=4, c=12)
            mreg = mall.reg(None, (l * 36, (l + 1) * 36))
            ts(dst[:, :, :, 0], src[:, :, :, 0], sel.ap[:, 12:13], None, ALU.mult, None, [mreg, sel.reg()], [modv.reg(l)])
            stt(dst[:, :, :, 0], src[:, :, :, 1], sel.ap[:, 13:14], dst[:, :, :, 0], ALU.mult, ALU.add,
                [mreg, sel.reg(), modv.reg(l)], [modv.reg(l)])
            S.add('dve', lambda e, o=dst[:, :, :, 1], i=src[:, :, :, 2]: e.tensor_copy(o, i), [mreg], [modv.reg(l)])
            stt(A1[:, l, :], modv[:, l, 16:32], 1.0, g2.ap[:, l * 16:(l + 1) * 16], ALU.add, ALU.mult,
                [modv.reg(l), g2.reg()], [A1.reg(l)])
            stt(A2[:, l, :], modv[:, l, 64:80], 1.0, g2.ap[:, (4 + l) * 16:(5 + l) * 16], ALU.add, ALU.mult,
                [modv.reg(l), g2.reg()], [A2.reg(l)])

    def mcol(l, kind, k, v):
        c = kind * 16 + k * 2 + v
        return modv[:, l, c:c + 1]

    def norm_tile(t0, v, Acol, Bcol, Areg, Breg, out_fn):
        nm = AR.mark()
        rstd = AR.alloc([512], F32)
        sq = AR.alloc([8, 512], BF16)
        ntmp = AR.alloc([2, 512], F32)
        act(sq.ap, x[:, :, t0:t0 + 512], AF.Square, [x.reg(None, (t0, t0 + 512))], [sq.reg()])
        lo, preg = psalloc(512)
        for k in range(8):
            mm(PS(lo, 512), preg, ones_bf, cbf_reg, sq[:, k, :], sq.reg(k), k == 0, k == 7)
        act(rstd.ap, PS(lo, 512), AF.Ln, [preg, one_col_reg], [rstd.reg()], bias=eps_col, scale=1.0 / D)
        act(rstd.ap, rstd.ap, AF.Exp, [rstd.reg()], [rstd.reg()], scale=-0.5)
        for k in range(8):
            tb = ntmp[:, k % 2, :]
            treg = ntmp.reg(k % 2)
            stt(tb, x[:, k, t0:t0 + 512], Acol(k), rstd.ap, ALU.mult, ALU.mult,
                [x.reg(k, (t0, t0 + 512)), rstd.reg(), Areg], [treg])
            out_fn(k, tb, treg)
        AR.release(nm)

    def norm_to_h(l, which):
        Av = A1 if which == 1 else A2
        shk = 0 if which == 1 else 3
        nm = AR.mark()
        tiles = ((0, 0), (1, 512))
        rst = [AR.alloc([512], F32) for _ in tiles]
        sqs = [AR.alloc([8, 512], BF16) for _ in tiles]
        tmp = [AR.alloc([2, 512], F32) for _ in tiles]
        pss = []
        for ti, (v, t0) in enumerate(tiles):
            act(sqs[ti].ap, x[:, :, t0:t0 + 512], AF.Square, [x.reg(None, (t0, t0 + 512))], [sqs[ti].reg()])
        for ti, (v, t0) in enumerate(tiles):
            lo, preg = psalloc(512)
            pss.append((lo, preg))
            for k in range(8):
                mm(PS(lo, 512), preg, ones_bf, cbf_reg, sqs[ti][:, k, :], sqs[ti].reg(k), k == 0, k == 7)
        for ti, (v, t0) in enumerate(tiles):
            lo, preg = pss[ti]
            act(rst[ti].ap, PS(lo, 512), AF.Ln, [preg, one_col_reg], [rst[ti].reg()], bias=eps_col, scale=1.0 / D)
        for ti, (v, t0) in enumerate(tiles):
            act(rst[ti].ap, rst[ti].ap, AF.Exp, [rst[ti].reg()], [rst[ti].reg()], scale=-0.5)
        for k in range(8):
            for ti, (v, t0) in enumerate(tiles):
                tb = tmp[ti][:, k % 2, :]
                treg = tmp[ti].reg(k % 2)
                stt(tb, x[:, k, t0:t0 + 512], Av[:, l, k * 2 + v:k * 2 + v + 1], rst[ti].ap, ALU.mult, ALU.mult,
                    [x.reg(k, (t0, t0 + 512)), rst[ti].reg(), Av.reg(l)], [treg])
                act(h[:, k, t0:t0 + 512], tb, AF.Identity, [treg, modv.reg(l)], [h.reg(k, (t0, t0 + 512))],
                    bias=mcol(l, shk, k, v))
        AR.release(nm)

    def final_norm_tile(ti):
        v, t0 = ((0, 0), (1, 512))[ti]

        def outf(k, tb, treg):
            key = ('d:yT%d' % len(y_keys), 0, 1)
            y_keys.append(key)
            dma('sp', d_yT[:, k * NT + t0: k * NT + t0 + 512], tb, [treg], [key])
        norm_tile(t0, v, lambda k: g2.ap[:, 8 * 16 + k * 2: 8 * 16 + k * 2 + 1], None, g2.reg(), None, outf)

    def resid_update(l, gkind, dout, v, t0, plo, preg):
        stt(x[:, dout, t0:t0 + 512], PS(plo, 512), mcol(l, gkind, dout, v), x[:, dout, t0:t0 + 512],
            ALU.mult, ALU.add, [preg, modv.reg(l), x.reg(dout, (t0, t0 + 512))], [x.reg(dout, (t0, t0 + 512))])

    def ffn_layer(l, final=False):
        m = AR.mark()
        norm_to_h(l, 2)
        asS = [AR.alloc([528], F32) for _ in range(2)]
        asP = [AR.alloc([520], F32) for _ in range(2)]
        cb = [AR.alloc([1024], F32) for _ in range(2)]
        J = 6
        actb = [AR.alloc([J, 1024], BF16) for _ in range(2)]
        NF = 12
        fring = AR.alloc([NF, 1024], BF16)
        fst = {'p': 0}

        def fload(dram_ap):
            p = fst['p']
            fst['p'] = (p + 1) % NF
            off = fring.off + p * 2048
            v = View(arena_t, off, [1024], BF16)
            dma('pool', v.ap, dram_ap, reads=[], writes=[('sb', off, off + 2048)])
            return v
        for b_ in asP + asS:
            S.add('dve', lambda e, b_=b_: e.memset(b_.ap, 0.0), [], [b_.reg()])
        groups = [list(range(0, 6)), list(range(6, 12)), list(range(12, 17)), list(range(17, 22))]
        for gi, grp in enumerate(groups):
            ab = actb[gi % 2]
            wds = []
            for jj, j in enumerate(grp):
                wu = wload(d_wup[l * NJ + j, :, :].rearrange('p (k n) -> p k n', k=8), [8, 256])
                wd = fload(d_wdn[l * NJ + j, :, :])
                wds.append(wd)
                aS = asS[j % 2]; aP = asP[j % 2]; c_ = cb[j % 2]
                cwb = (l * NJ + j) * 4
                w0 = cw.ap[:, cwb:cwb + 1]; w1 = cw.ap[:, cwb + 1:cwb + 2]; w2 = cw.ap[:, cwb + 2:cwb + 3]; bb = cw.ap[:, cwb + 3:cwb + 4]
                loA, rA = psalloc(512)
                for k in range(8):
                    mm(PS(loA, 512), rA, wu[:, k, 0:128], wu.reg(k, (0, 128)), h[:, k, 0:512], h.reg(k, (0, 512)), k == 0, k == 7)
                loAP, rAP = psalloc(512)
                for k in range(8):
                    mm(PS(loAP, 512), rAP, wu[:, k, 0:128], wu.reg(k, (0, 128)), h[:, k, 512:1024], h.reg(k, (512, 1024)), k == 0, k == 7)
                act(aS.ap[:, 1:521].rearrange('p (s t) -> p s t', s=8)[:, :, 0:64],
                    PS(loA, 512).rearrange('p (s t) -> p s t', s=8), AF.Copy, [rA], [aS.reg((1, 521))])
                act(c_.ap[:, 0:512], PS(loA, 512), AF.Identity, [rA, cw.reg()], [c_.reg((0, 512))], bias=bb, scale=w1)
                act(aP.ap[:, 1:515].rearrange('p (s t) -> p s t', s=2)[:, :, 0:256],
                    PS(loAP, 512).rearrange('p (s t) -> p s t', s=2), AF.Copy, [rAP], [aP.reg((1, 515))])
                act(c_.ap[:, 512:1024], PS(loAP, 512), AF.Identity, [rAP, cw.reg()], [c_.reg((512, 1024))], bias=bb, scale=w1)
                loG, rG = psalloc(512)
                for k in range(8):
                    mm(PS(loG, 512), rG, wu[:, k, 128:256], wu.reg(k, (128, 256)), h[:, k, 0:512], h.reg(k, (0, 512)), k == 0, k == 7)
                loGP, rGP = psalloc(512)
                for k in range(8):
                    mm(PS(loGP, 512), rGP, wu[:, k, 128:256], wu.reg(k, (128, 256)), h[:, k, 512:1024], h.reg(k, (512, 1024)), k == 0, k == 7)
                cS = c_.ap[:, 0:512].rearrange('p (s t) -> p s t', s=8)

                def sv(o):
                    return aS.ap[:, o:o + 520].rearrange('p (s t) -> p s t', s=8)[:, :, 0:64]
                stt(cS, sv(0), w0, cS, ALU.mult, ALU.add, [aS.reg(), cw.reg(), c_.reg((0, 512))], [c_.reg((0, 512))])
                stt(cS, sv(2), w2, cS, ALU.mult, ALU.add, [aS.reg(), cw.reg(), c_.reg((0, 512))], [c_.reg((0, 512))])
                cP = c_.ap[:, 512:1024].rearrange('p (s t) -> p s t', s=2)

                def pv(o):
                    return aP.ap[:, o:o + 514].rearrange('p (s t) -> p s t', s=2)[:, :, 0:256]
                stt(cP, pv(0), w0, cP, ALU.mult, ALU.add, [aP.reg(), cw.reg(), c_.reg((512, 1024))], [c_.reg((512, 1024))])
                stt(cP, pv(2), w2, cP, ALU.mult, ALU.add, [aP.reg(), cw.reg(), c_.reg((512, 1024))], [c_.reg((512, 1024))])
                act(c_.ap, c_.ap, AF.Silu, [c_.reg()], [c_.reg()])
                tt(ab[:, jj, 0:512], c_.ap[:, 0:512], PS(loG, 512), ALU.mult, [c_.reg((0, 512)), rG], [ab.reg(jj, (0, 512))])
                tt(ab[:, jj, 512:1024], c_.ap[:, 512:1024], PS(loGP, 512), ALU.mult, [c_.reg((512, 1024)), rGP], [ab.reg(jj, (512, 1024))])
                if l == 0 and NLAYERS > 1 and j < 12:
                    for i_ in range(12 + 3 * j, 15 + 3 * j):
                        mod_block(i_)
                    if j == 11:
                        mod_launch(1)
            for ti, (v, t0) in enumerate(((0, 0), (1, 512))):
                for dout in range(8):
                    loY, rY = psalloc(512)
                    for jj in range(len(grp)):
                        wd = wds[jj]
                        mm(PS(loY, 512), rY, wd.ap[:, dout * 128:(dout + 1) * 128], wd.reg((dout * 128, (dout + 1) * 128)),
                           ab[:, jj, t0:t0 + 512], ab.reg(jj, (t0, t0 + 512)), jj == 0, jj == len(grp) - 1)
                    resid_update(l, 5, dout, v, t0, loY, rY)
                if final and gi == len(groups) - 1:
                    final_norm_tile(ti)
        AR.release(m)

    def fnet_layer(l):
        j = l // 2
        m = AR.mark()
        posL = AR.alloc([2, 16, 512], BF16)
        dma('sp', posL.ap.rearrange('p a b c -> p (a b c)'), d_posL[:, :], [], [posL.reg()])
        norm_to_h(l, 1)
        m1 = AR.mark()
        uT = AR.alloc([8, NT], BF16)
        ucsP = AR.alloc([4, 2048], BF16)
        wins = [wload(d_fwin[j * 8 + c, :, :].rearrange('p (k n) -> p k n', k=8), [8, 128]) for c in range(8)]

        def uproj(t0):
            for c in range(8):
                w = wins[c]
                lo, pr = psalloc(512)
                for k in range(8):
                    mm(PS(lo, 512), pr, w[:, k, :], w.reg(k), h[:, k, t0:t0 + 512], h.reg(k, (t0, t0 + 512)), k == 0, k == 7)
                act(uT[:, c, t0:t0 + 512], PS(lo, 512), AF.Copy, [pr], [uT.reg(c, (t0, t0 + 512))])

        def chan_dft(t0, dst):
            for tc in range(4):
                for gq in range(4):
                    for (mat, off) in ((chC, 0), (chS, 1024)):
                        lo, pr = psalloc(256)
                        for kk in range(2):
                            cols = (t0 + tc * 128, t0 + (tc + 1) * 128)
                            mm(PS(lo, 256), pr, uT[:, 2 * gq + kk, cols[0]:cols[1]], uT.reg(2 * gq + kk, cols),
                               mat[:, kk, :], cbf_reg, kk == 0, kk == 1)
                        o0 = off + gq * 256
                        if (gq + (off > 0)) % 2 == 0:
                            act(dst[:, tc, o0:o0 + 256], PS(lo, 256), AF.Copy, [pr], [dst.reg(tc, (o0, o0 + 256))])
                        else:
                            S.add('dve', lambda e, o=dst[:, tc, o0:o0 + 256], i=PS(lo, 256): e.tensor_copy(o, i),
                                  [pr], [dst.reg(tc, (o0, o0 + 256))])

        def wo_proj(wos, v, t0):
            for dout in range(8):
                w = wos[dout]
                lo, pr = psalloc(512)
                for k in range(8):
                    mm(PS(lo, 512), pr, w[:, k, :], w.reg(k), h[:, k, t0:t0 + 512], h.reg(k, (t0, t0 + 512)), k == 0, k == 7)
                resid_update(l, 2, dout, v, t0, lo, pr)

        uproj(0)
        dma('sp', ag_h_in[:, :].rearrange('(k p) t -> p k t', p=128), uT[:, :, 0:512], [uT.reg(None, (0, 512))], [dreg(ag_h_in)])
        allgather(ag_h_in, ag_h_out)
        uproj(512)
        chan_dft(512, ucsP)
        for s in range(2):
            for c in range(8):
                lo, pr = psalloc(256)
                n = 0
                for tcl in range(2):
                    tc = s * 2 + tcl
                    for (off, mat) in ((0, pC), (1024, pSn)):
                        mm(PS(lo, 256), pr, ucsP[:, tc, off + c * 128: off + (c + 1) * 128], ucsP.reg(tc, (off + c * 128, off + (c + 1) * 128)),
                           mat[:, tcl, :], cbf_reg, n == 0, n == 3)
                        n += 1
                t0 = 512 + s * 256
                act(h[:, c, t0:t0 + 256], PS(lo, 256), AF.Copy, [pr], [h.reg(c, (t0, t0 + 256))])
        wos = [wload(d_fwo[j * 8 + dout, :, :].rearrange('p (k n) -> p k n', k=8), [8, 128]) for dout in range(8)]
        wo_proj(wos, 1, 512)
        AR.release(m1)
        ufull = AR.alloc([2, 16, 1024], BF16)
        upc = [AR.alloc([8, 512], BF16) for _ in range(2)]
        ncp = 0
        for r in range(4):
            ub = upc[r % 2]
            dma('sp', ub.ap, ag_h_out[r * 1024:(r + 1) * 1024, :].rearrange('(k p) t -> p k t', p=128), [dreg(ag_h_out)], [ub.reg()])
            for tcl in range(4):
                tc = r * 4 + tcl
                for gq in range(4):
                    for (mat, pi) in ((chC, 0), (chS, 1)):
                        lo, pr = psalloc(256)
                        for kk in range(2):
                            mm(PS(lo, 256), pr, ub[:, 2 * gq + kk, tcl * 128:(tcl + 1) * 128], ub.reg(2 * gq + kk, (tcl * 128, (tcl + 1) * 128)),
                               mat[:, kk, :], cbf_reg, kk == 0, kk == 1)
                        dst_ap = ufull[:, pi, tc, gq * 256:(gq + 1) * 256]
                        dst_rg = ufull.reg(pi, tc, (gq * 256, (gq + 1) * 256))
                        if ncp % 2 == 0:
                            act(dst_ap, PS(lo, 256), AF.Copy, [pr], [dst_rg])
                        else:
                            S.add('dve', lambda e, o=dst_ap, i=PS(lo, 256): e.tensor_copy(o, i), [pr], [dst_rg])
                        ncp += 1
        for c in range(8):
            lo, pr = psalloc(512)
            n = 0
            for tc in range(16):
                for pi in range(2):
                    mm(PS(lo, 512), pr, ufull[:, pi, tc, c * 128:(c + 1) * 128], ufull.reg(pi, tc, (c * 128, (c + 1) * 128)),
                       posL[:, pi, tc, :], posL.reg(pi, tc), n == 0, n == 31)
                    n += 1
            act(h[:, c, 0:512], PS(lo, 512), AF.Copy, [pr], [h.reg(c, (0, 512))])
        wo_proj(wos, 0, 0)
        AR.release(m)

    def gla_head(l, hT, hreg_fn, seqs, W, wg, gcol, o_dst, sbf_alias=None, after_phase1=None):
        j = l // 2
        m = AR.mark()
        ntok = sum(s['nch'] for s in seqs) * 128
        nchT = ntok // 128
        glr = AR.alloc([512], F32)
        sr = AR.alloc([2, ntok], BF16)
        qt = [AR.alloc([ntok], BF16) for _ in range(2)]
        kt = [AR.alloc([ntok], BF16) for _ in range(2)]
        kh = [AR.alloc([nchT, 128], BF16) for _ in range(2)]
        vt = AR.alloc([nchT, 256], BF16)
        dec = AR.alloc([2, nchT], F32)
        if sbf_alias is None:
            Sbf = [AR.alloc([nchT, 256], BF16) for _ in range(2)]
        else:
            Sbf = [View(arena_t, sbf_alias.off + z * nchT * 512, [nchT, 256], BF16) for z in range(2)]
        SstAll = [[[AR.alloc([256], F32) for _ in range(2)] for _ in range(2)] for _ in seqs]
        Lb = [AR.alloc([2, 128], F32) for _ in range(2)]
        etmp = [AR.alloc([2, 128], F32) for _ in range(2)]
        eqk = [AR.alloc([4, 128], F32) for _ in range(2)]
        ekh = [AR.alloc([2, 128], F32) for _ in range(2)]
        att = [AR.alloc([2, 128], BF16) for _ in range(2)]
        osq = [AR.alloc([2, 128], BF16) for _ in range(2)]
        orst = [AR.alloc([128], F32) for _ in range(2)]
        otmp = [AR.alloc([2, 128], F32) for _ in range(2)]
        S.add('dve', lambda e: e.memset(glr.ap[32:33, :], 1.0), [], [glr.reg()])
        loc = 0
        for s in seqs:
            s['loc'] = loc
            loc += s['nch'] * 128
        items = []
        for s in seqs:
            nt_ = s['nch'] * 128
            for b0 in range(0, nt_, 512):
                bw = min(512, nt_ - b0)
                for c in range(b0 // 128, (b0 + bw) // 128):
                    items.append((s, b0, bw, c, c == b0 // 128))

        def rpass(s, b0, bw):
            c0, lc = s['c0'], s['loc']
            for dvc in range(2):
                lo, pr = psalloc(bw)
                for k in range(8):
                    mm(PS(lo, bw), pr, W[:, k, 512 + dvc * 128: 640 + dvc * 128], W.reg(k, (512 + dvc * 128, 640 + dvc * 128)),
                       hT(k, c0 + b0, c0 + b0 + bw), hreg_fn(k, c0 + b0, c0 + b0 + bw), k == 0, k == 7)
                act(sr[:, dvc, lc + b0: lc + b0 + bw], PS(lo, bw), AF.Silu, [pr], [sr.reg(dvc, (lc + b0, lc + b0 + bw))])

        glr_all = (ntok <= 512)

        def prologue(s, b0, bw):
            c0, lc = s['c0'], s['loc']
            d0 = (lc + b0) if glr_all else 0
            lo, pr = psalloc(bw)
            for k in range(8):
                mm(PS(lo, bw, 32), pr, W[:, k, 768:800], W.reg(k, (768, 800)), hT(k, c0 + b0, c0 + b0 + bw), hreg_fn(k, c0 + b0, c0 + b0 + bw), k == 0, k == 7)
            act(glr.ap[0:32, d0:d0 + bw], PS(lo, bw, 32), AF.Copy, [pr], [glr.reg((d0, d0 + bw))])

        QB, KB = 6 * 512, 7 * 512
        regQB = ('ps', QB * 4, (QB + 512) * 4)
        regKB = ('ps', KB * 4, (KB + 512) * 4)

        def blockP(s, b0, bw):
            c0 = s['c0']
            for k in range(8):
                mm(PS(QB, bw), regQB, W[:, k, 0:128], W.reg(k, (0, 128)), hT(k, c0 + b0, c0 + b0 + bw), hreg_fn(k, c0 + b0, c0 + b0 + bw), k == 0, k == 7)
            for k in range(8):
                mm(PS(KB, bw), regKB, W[:, k, 128:256], W.reg(k, (128, 256)), hT(k, c0 + b0, c0 + b0 + bw), hreg_fn(k, c0 + b0, c0 + b0 + bw), k == 0, k == 7)

        def stageP(s, b0, bw, c):
            c0 = s['c0']
            a0, a1 = c0 + c * 128, c0 + (c + 1) * 128
            loKV, prKV = psalloc(384)
            for k in range(8):
                mm(PS(loKV, 384), prKV, hT(k, a0, a1), hreg_fn(k, a0, a1), W[:, k, 128:512], W.reg(k, (128, 512)), k == 0, k == 7)
            return (loKV, prKV)

        def stageA(s, b0, bw, c):
            lc = s['loc']
            ci = (lc // 128) + c
            g0, g1 = c * 128 - b0, (c + 1) * 128 - b0
            if glr_all:
                g0, g1 = lc + c * 128, lc + (c + 1) * 128
            pb = ci % 2
            loL, prL = psalloc(256)
            for z in range(2):
                mm(PS(loL + z * 128, 128), prL, glr.ap[0:33, g0:g1], glr.reg((g0, g1)), wg.ap[0:33, z * 128:(z + 1) * 128], wg.reg(), True, True)
            act(etmp[pb].ap, PS(loL, 256).rearrange('p (a b) -> p a b', a=2), AF.Exp, [prL], [etmp[pb].reg()], scale=-1.0)
            act(Lb[pb].ap, etmp[pb].ap, AF.Ln, [etmp[pb].reg(), one_col_reg], [Lb[pb].reg()], bias=one_col)
            loC, prC = psalloc(512)
            for z in range(2):
                tri, trireg = cf(z)
                mm(PS(loC + z * 128, 128), prC, Lb[pb][:, z, :], Lb[pb].reg(z), tri, trireg, True, True)
            for z in range(2):
                tri2, tri2reg = cf(2 + z)
                mm(PS(loC + 256 + z * 128, 128), prC, tri2, tri2reg, Lb[pb][:, z, :], Lb[pb].reg(z), True, True)
            return (loC, prC)

        def stageB(s, b0, bw, c, ctx, pctx):
            loC, prC = ctx
            loKV, prKV = pctx
            qo = c * 128 - b0
            c0, lc = s['c0'], s['loc']
            ci = (lc // 128) + c
            l0, l1 = lc + c * 128, lc + (c + 1) * 128
            pb = ci % 2
            cview = PS(loC, 256).rearrange('p (a b) -> p a b', a=2)
            act(eqk[pb][:, 0:2, :], cview, AF.Exp, [prC], [eqk[pb].reg((0, 2))], scale=-1.0 / 16)
            act(eqk[pb][:, 2:4, :], cview, AF.Exp, [prC], [eqk[pb].reg((2, 4))], scale=1.0 / 16)
            act(ekh[pb].ap, PS(loC + 256, 256).rearrange('p (a b) -> p a b', a=2), AF.Exp, [prC], [ekh[pb].reg()], scale=-1.0 / 16)
            act(dec[:, :, ci], PS(loC + 127, 2), AF.Exp, [prC], [dec.reg(None, ci)], scale=-1.0 / 16)
            for z in range(2):
                stt(qt[z].ap[:, l0:l1], PS(QB + qo, 128), 128.0 ** -0.5, eqk[pb][:, z, :], ALU.mult, ALU.mult,
                    [regQB, eqk[pb].reg(z)], [qt[z].reg((l0, l1))])
                tt(kt[z].ap[:, l0:l1], PS(KB + qo, 128), eqk[pb][:, 2 + z, :], ALU.mult,
                   [regKB, eqk[pb].reg(2 + z)], [kt[z].reg((l0, l1))])
                tt(kh[z][:, ci, :], PS(loKV, 128), ekh[pb][:, z, :], ALU.mult, [prKV, ekh[pb].reg(z)], [kh[z].reg(ci)])
            act(vt[:, ci, :], PS(loKV + 128, 256), AF.Copy, [prKV], [vt.reg(ci)])

        for s in seqs:
            nt_ = s['nch'] * 128
            for b0 in range(0, nt_, 512):
                rpass(s, b0, min(512, nt_ - b0))
                if glr_all:
                    prologue(s, b0, min(512, nt_ - b0))
        ps_state['nrot'] = 6
        ctxs = {}
        pctxs = {}
        for idx, it in enumerate(items):
            if it[4]:
                if not glr_all:
                    prologue(it[0], it[1], it[2])
                blockP(it[0], it[1], it[2])
                pctxs[idx] = stageP(*it[:4])
                ctxs[idx] = stageA(*it[:4])
            nx = items[idx + 1] if idx + 1 < len(items) else None
            if nx is not None and not nx[4]:
                pctxs[idx + 1] = stageP(*nx[:4])
                ctxs[idx + 1] = stageA(*nx[:4])
            stageB(it[0], it[1], it[2], it[3], ctxs.pop(idx), pctxs.pop(idx))
        ps_state['nrot'] = 8
        if after_phase1 is not None:
            after_phase1()
        for si, s in enumerate(seqs):
            nch, lc = s['nch'], s['loc']
            cbase = lc // 128
            Sst = SstAll[si]
            cur = [None, None]
            for z in range(2):
                if s['init'] is not None:
                    dma('sp', Sst[z][0].ap, s['init'][z], [], [Sst[z][0].reg()])
                    cur[z] = 0
            for n_ in range(nch):
                for z in range(2):
                    c = n_ if z == 0 else nch - 1 - n_
                    ci = cbase + c
                    if cur[z] is not None:
                        st = Sst[z][cur[z]]
                        act(Sbf[z][:, ci, :], st.ap, AF.Copy, [st.reg()], [Sbf[z].reg(ci)])
                    else:
                        S.add('dve', lambda e, o=Sbf[z][:, ci, :]: e.memset(o, 0.0), [], [Sbf[z].reg(ci)])
                    last = (n_ == nch - 1)
                    if last and s['fin'] is None:
                        continue
                    lo, pr = psalloc(256)
                    mm(PS(lo, 256), pr, kh[z][:, ci, :], kh[z].reg(ci), vt[:, ci, :], vt.reg(ci), True, True)
                    if cur[z] is not None:
                        st = Sst[z][cur[z]]
                        nst = Sst[z][1 - cur[z]]
                        stt(nst.ap, st.ap, dec[:, z, ci:ci + 1], PS(lo, 256), ALU.mult, ALU.add,
                            [st.reg(), dec.reg(z, ci), pr], [nst.reg()])
                        cur[z] = 1 - cur[z]
                    else:
                        nst = Sst[z][0]
                        S.add('dve', lambda e, o=nst.ap, i=PS(lo, 256): e.tensor_copy(o, i), [pr], [nst.reg()])
                        cur[z] = 0
            if s['fin'] is not None:
                for z in range(2):
                    st = Sst[z][cur[z]]
                    key = ('d:ns%d' % len(ns_keys), 0, 1)
                    ns_keys.append(key)
                    dma('sp', s['fin'][z], st.ap, [st.reg()], [key])
        chunks = []
        for s in seqs:
            for c in range(s['nch']):
                chunks.append((s, c))

        def emit_att(s, c):
            lc = s['loc']
            ci = (lc // 128) + c
            l0, l1 = lc + c * 128, lc + (c + 1) * 128
            pb = ci % 2
            loA, prA = psalloc(256)
            for z in range(2):
                mm(PS(loA + z * 128, 128), prA, kt[z].ap[:, l0:l1], kt[z].reg((l0, l1)), qt[z].ap[:, l0:l1], qt[z].reg((l0, l1)), True, True)
            tt(att[pb].ap, PS(loA, 256).rearrange('p (a b) -> p a b', a=2), cbf.ap[:, 0:256].rearrange('p (a b) -> p a b', a=2),
               ALU.mult, [prA, cbf_reg], [att[pb].reg()])

        if chunks:
            emit_att(*chunks[0])
        for idx, (s, c) in enumerate(chunks):
            c0, lc = s['c0'], s['loc']
            ci = (lc // 128) + c
            l0, l1 = lc + c * 128, lc + (c + 1) * 128
            pb = ci % 2
            if idx + 1 < len(chunks):
                emit_att(*chunks[idx + 1])
            loO, prO = psalloc(256)
            for dvc in range(2):
                o_ps = PS(loO + dvc * 128, 128)
                mm(o_ps, prO, vt[:, ci, dvc * 128:(dvc + 1) * 128], vt.reg(ci), att[pb][:, 0, :], att[pb].reg(0), True, False)
                mm(o_ps, prO, vt[:, ci, dvc * 128:(dvc + 1) * 128], vt.reg(ci), att[pb][:, 1, :], att[pb].reg(1), False, False)
                mm(o_ps, prO, Sbf[0][:, ci, dvc * 128:(dvc + 1) * 128], Sbf[0].reg(ci), qt[0].ap[:, l0:l1], qt[0].reg((l0, l1)), False, False)
                mm(o_ps, prO, Sbf[1][:, ci, dvc * 128:(dvc + 1) * 128], Sbf[1].reg(ci), qt[1].ap[:, l0:l1], qt[1].reg((l0, l1)), False, True)
            oview = PS(loO, 256).rearrange('p (a b) -> p a b', a=2)
            act(osq[pb].ap, oview, AF.Square, [prO], [osq[pb].reg()])
            loS, prS = psalloc(128)
            for dvc in range(2):
                mm(PS(loS, 128), prS, ones_bf, cbf_reg, osq[pb][:, dvc, :], osq[pb].reg(dvc), dvc == 0, dvc == 1)
            act(orst[pb].ap, PS(loS, 128), AF.Ln, [prS, one_col_reg], [orst[pb].reg()], bias=eps_col, scale=1.0 / 256)
            act(orst[pb].ap, orst[pb].ap, AF.Exp, [orst[pb].reg()], [orst[pb].reg()], scale=-0.5)
            for dvc in range(2):
                stt(otmp[pb][:, dvc, :], PS(loO + dvc * 128, 128), gcol(dvc), orst[pb].ap, ALU.mult, ALU.mult,
                    [prO, orst[pb].reg(), gng.reg()], [otmp[pb].reg(dvc)])
                dap, dr = o_dst(c0 + c * 128, dvc)
                tt(dap, otmp[pb][:, dvc, :], sr[:, dvc, l0:l1], ALU.mult, [otmp[pb].reg(dvc), sr.reg(dvc, (l0, l1))], [dr])
        AR.release(m)

    def gla_layer(l):
        j = l // 2
        m = AR.mark()
        Wn = {}

        def prefetch(key, dram):
            Wn[key] = wload(dram.rearrange('p (k n) -> p k n', k=8), [8, 800])
        prefetch(0, d_gwin[j * 4 + 0, :, :])
        norm_to_h(l, 1)
        oS = AR.alloc([2, 2048], BF16)
        oall = AR.alloc([8, NT], BF16)
        wgv = AR.alloc([256], F32)
        hfull = AR.alloc([8, 2048], BF16)
        dma('sp', ag_h_in[:, :].rearrange('(k p) t -> p k t', p=128), h[:, :, 0:512], [h.reg(None, (0, 512))], [dreg(ag_h_in)])
        allgather(ag_h_in, ag_h_out)

        def run_prompt_head(hd, nxt, load_wg=True):
            W = Wn[hd]
            if load_wg:
                dma('sp', wgv.ap[0:33, :], d_wg2[j * 4 + hd, :, :], [], [wgv.reg()])
            seqs = []
            for s in range(2):
                fin = tuple(d_ns[((s * 2 + j) * 2 + z) * 4 + hd, :, :] for z in range(2))
                seqs.append(dict(c0=512 + s * 256, nch=2, init=None, fin=fin))
            gla_head(l, lambda k, a, b: h[:, k, a:b], lambda k, a, b: h.reg(k, (a, b)), seqs, W, wgv,
                     lambda dvc: gng.ap[:, j * 2 + dvc: j * 2 + dvc + 1],
                     lambda ca, dvc, hd=hd: (oall[:, hd * 2 + dvc, ca:ca + 128], oall.reg(hd * 2 + dvc, (ca, ca + 128))),
                     after_phase1=nxt)

        run_prompt_head(0, lambda: prefetch(1, d_gwin[j * 4 + 1, :, :]))
        dma('sp', wgv.ap[0:33, :], d_wg2[j * 4 + 1, :, :], [], [wgv.reg()])
        for r in range(4):
            dma('sp', hfull[:, :, r * 512:(r + 1) * 512], ag_h_out[r * 1024:(r + 1) * 1024, :].rearrange('(k p) t -> p k t', p=128),
                [dreg(ag_h_out)], [hfull.reg(None, (r * 512, (r + 1) * 512))])
        run_prompt_head(1, lambda: prefetch('s', d_gwins[j, :, :]), load_wg=False)
        dma('sp', wgv.ap[0:33, :], d_wg2s[j, :, :], [], [wgv.reg()])
        init = (d_st0[j * 2 + 0, :, :], d_st0[j * 2 + 1, :, :])
        gla_head(l, lambda k, a, b: hfull[:, k, a:b], lambda k, a, b: hfull.reg(k, (a, b)),
                 [dict(c0=0, nch=16, init=init, fin=None)], Wn['s'], wgv,
                 lambda dvc: gng.ap[:, j * 2 + dvc: j * 2 + dvc + 1],
                 lambda ca, dvc: (oS[:, dvc, ca:ca + 128], oS.reg(dvc, (ca, ca + 128))), sbf_alias=hfull,
                 after_phase1=lambda: prefetch(2, d_gwin[j * 4 + 2, :, :]))
        dma('sp', ag_o_in[:, :].rearrange('(c p) t -> p c t', p=128), oS.ap, [oS.reg()], [dreg(ag_o_in)])
        allgather(ag_o_in, ag_o_out)
        run_prompt_head(2, lambda: prefetch(3, d_gwin[j * 4 + 3, :, :]))
        run_prompt_head(3, None)
        wos = [wload(d_gwo[j * 8 + dout, :, :].rearrange('p (k n) -> p k n', k=8), [8, 128]) for dout in range(8)]

        def wo_proj(v, t0):
            for dout in range(8):
                w = wos[dout]
                lo, pr = psalloc(512)
                for k in range(8):
                    mm(PS(lo, 512), pr, w[:, k, :], w.reg(k), oall[:, k, t0:t0 + 512], oall.reg(k, (t0, t0 + 512)), k == 0, k == 7)
                resid_update(l, 2, dout, v, t0, lo, pr)
        wo_proj(1, 512)
        oblk = AR.alloc([8, 4, 512], BF16)
        dma('sp', oblk.ap, ag_o_out[:, :].rearrange('(c p) (qq t) -> p c qq t', p=128, qq=4), [dreg(ag_o_out)], [oblk.reg()])
        for qq in range(4):
            for c in range(8):
                if qq == 0:
                    ts(oall[:, c, 0:512], oblk[:, c, qq, :], sel.ap[:, qq:qq + 1], None, ALU.mult, None,
                       [oblk.reg(c, qq), sel.reg()], [oall.reg(c, (0, 512))])
                else:
                    stt(oall[:, c, 0:512], oblk[:, c, qq, :], sel.ap[:, qq:qq + 1], oall[:, c, 0:512], ALU.mult, ALU.add,
                        [oblk.reg(c, qq), sel.reg(), oall.reg(c, (0, 512))], [oall.reg(c, (0, 512))])
        wo_proj(0, 0)
        AR.release(m)

    for i_ in range(12):
        mod_block(i_)
    mod_launch(0)
    mod_finish(0)
    for l in range(NLAYERS):
        if l % 2 == 0:
            if 'gla' in FLAGS:
                gla_layer(l)
        else:
            if 'fnet' in FLAGS:
                fnet_layer(l)
        if 'ffn' in FLAGS:
            ffn_layer(l, final=(l == NLAYERS - 1))
        if l == 0 and NLAYERS > 1:
            mod_finish(1)
    if not ('ffn' in FLAGS and NLAYERS >= 1):
        for ti in range(2):
            final_norm_tile(ti)
    S.add('sp', None, reads=y_keys + ns_keys, writes=[])

    S.finalize(nc, stack)
    with stack:
        with nc.Block() as block:
            @block.tensor
            def _(e):
                S.emit('pe', e)

            @block.scalar
            def _(e):
                S.emit('act', e)

            @block.vector
            def _(e):
                S.emit('dve', e)

            @block.gpsimd
            def _(e):
                S.emit('pool', e)

            @block.sync
            def _(e):
                S.emit('sp', e)
    return nc


def _fm(a):
    T = a.shape[0]
    return np.ascontiguousarray(a.reshape(T, 8, 128).transpose(2, 1, 0)).reshape(128, 8 * T)


def _vec(v):
    return np.ascontiguousarray(v.reshape(-1, 128).T)


def _wblk(w, ncol):
    K, N = w.shape
    nk = K // 128
    return np.ascontiguousarray(w.reshape(nk, 128, N // ncol, ncol).transpose(2, 1, 0, 3)).reshape(N // ncol, 128, nk * ncol)


def _consts(q):
    j = np.arange(128)[:, None]
    i = np.arange(128)[None, :]
    triF = (j <= i).astype(np.float32)
    triB = (j >= i).astype(np.float32)
    triF2 = (j > i).astype(np.float32)
    triB2 = (j < i).astype(np.float32)
    cf32 = np.concatenate([triF, triB, triF2, triB2, np.ones((128, 4), np.float32)], axis=1)
    cf32[:, 513] = EPS
    bf = ml_dtypes.bfloat16
    n = np.arange(256)
    ang = 2 * np.pi * np.outer(n, n) / 256.0
    C = np.cos(ang); Sn = np.sin(ang)

    def two(mat):
        return mat.reshape(2, 128, 256).transpose(1, 0, 2).reshape(128, 512)
    cbf = np.concatenate([triF, triB, np.ones((128, 128)), two(C), two(Sn), two(C / 256.0), two(-Sn / 256.0)], axis=1).astype(bf)
    t = np.arange(2048, dtype=np.float64)[:, None]
    tp = (q * 512 + np.arange(512, dtype=np.float64))[None, :]
    angL = 2 * np.pi * ((t * tp) % 2048) / 2048.0
    sc = 1.0 / np.sqrt(2048.0 * 256.0)
    CL = (np.cos(angL) * sc).reshape(16, 128, 512).transpose(1, 0, 2)
    SLn = (-np.sin(angL) * sc).reshape(16, 128, 512).transpose(1, 0, 2)
    posL = np.stack([CL, SLn], axis=1).reshape(128, 2 * 16 * 512).astype(bf)
    return cf32, np.ascontiguousarray(cbf), np.ascontiguousarray(posL)


_NC_CACHE = {}


def kernel(x_prompt, x_sample, state_gla, c, c_ctx, norm_mix_g, norm_ffn_g, w_mod, b_mod,
           gla_w_in, gla_w_g2, gla_b_g, gla_norm_g, gla_w_o, fnet_w_in, fnet_w_o,
           ffn_w_up, ffn_conv_w, ffn_conv_b, ffn_w_down, final_norm_g):
    f = lambda a: np.asarray(a, dtype=np.float32)
    x_prompt, x_sample, state_gla, c, c_ctx = map(f, (x_prompt, x_sample, state_gla, c, c_ctx))
    norm_mix_g, norm_ffn_g, w_mod, b_mod = map(f, (norm_mix_g, norm_ffn_g, w_mod, b_mod))
    gla_w_in, gla_w_g2, gla_b_g, gla_norm_g, gla_w_o = map(f, (gla_w_in, gla_w_g2, gla_b_g, gla_norm_g, gla_w_o))
    fnet_w_in, fnet_w_o, ffn_w_up, ffn_conv_w, ffn_conv_b, ffn_w_down, final_norm_g = map(
        f, (fnet_w_in, fnet_w_o, ffn_w_up, ffn_conv_w, ffn_conv_b, ffn_w_down, final_norm_g))

    wmod = np.stack([_wblk(w_mod[l], 128) for l in range(4)], axis=0)
    bmodv = np.stack([_vec(b_mod[l]) for l in range(4)], axis=1)
    cT = np.ascontiguousarray(np.stack([_vec(c[0]), _vec(c[1]), _vec(c_ctx)], axis=2).reshape(128, 24))
    gains = [norm_mix_g[l] for l in range(4)] + [norm_ffn_g[l] for l in range(4)] + [final_norm_g]
    g2 = np.concatenate([np.repeat(_vec(g), 2, axis=1) for g in gains], axis=1)
    wup = []
    for l in range(4):
        a = _wblk(ffn_w_up[l][:, :FFN], 128).reshape(NJ, 128, 8, 128)
        g = _wblk(ffn_w_up[l][:, FFN:], 128).reshape(NJ, 128, 8, 128)
        wup.append(np.concatenate([a, g], axis=3).reshape(NJ, 128, 2048))
    wup = np.ascontiguousarray(np.concatenate(wup, axis=0))
    wdn = np.ascontiguousarray(ffn_w_down.reshape(4 * NJ, 128, 1024))
    cw = np.concatenate([ffn_conv_w, ffn_conv_b[:, None, :]], axis=1)
    cw = np.ascontiguousarray(cw.reshape(4, 4, NJ, 128).transpose(3, 0, 2, 1)).reshape(128, 4 * NJ * 4)
    fwin = np.concatenate([_wblk(fnet_w_in[j], 128) for j in range(2)], axis=0)
    fwo = np.concatenate([_wblk(fnet_w_o[j], 128) for j in range(2)], axis=0)
    gwo = np.concatenate([_wblk(gla_w_o[j], 128) for j in range(2)], axis=0)
    gwin = np.zeros((2, 4, 128, 8, 800), np.float32)
    wg2 = np.zeros((2, 4, 33, 256), np.float32)
    for j in range(2):
        w = gla_w_in[j].reshape(8, 128, 3104).transpose(1, 0, 2)
        for hd in range(4):
            gwin[j, hd, :, :, 0:128] = w[:, :, hd * 128:(hd + 1) * 128]
            gwin[j, hd, :, :, 128:256] = w[:, :, 512 + hd * 128: 512 + (hd + 1) * 128]
            gwin[j, hd, :, :, 256:512] = w[:, :, 1024 + hd * 256: 1024 + (hd + 1) * 256]
            gwin[j, hd, :, :, 512:768] = w[:, :, 2048 + hd * 256: 2048 + (hd + 1) * 256]
            gwin[j, hd, :, :, 768:800] = w[:, :, 3072:3104]
            for z in range(2):
                wg2[j, hd, z * 16:(z + 1) * 16, z * 128:(z + 1) * 128] = gla_w_g2[j, z][:, hd * 128:(hd + 1) * 128]
                wg2[j, hd, 32, z * 128:(z + 1) * 128] = gla_b_g[j, z][hd * 128:(hd + 1) * 128]
    gwin = gwin.reshape(8, 128, 6400)
    wg2f = wg2.reshape(8, 33, 256)
    gng = np.concatenate([_vec(gla_norm_g[j]) for j in range(2)], axis=1)

    in_maps = []
    for i in range(8):
        b, q = i // 4, i % 4
        X = np.concatenate([x_sample[b, q * 512:(q + 1) * 512], x_prompt[2 * i], x_prompt[2 * i + 1]], axis=0)
        cf32, cbf, posL = _consts(q)
        sel = np.zeros((128, 14), np.float32)
        sel[:, 12 + b] = 1.0
        sel[:, q] = 1.0
        if q > 0:
            sel[:, 4 + q - 1] = 1.0
        if q < 3:
            sel[:, 8 + q + 1] = 1.0
        in_maps.append(dict(
            xT=_fm(X), cT=cT, wmodsh=np.ascontiguousarray(wmod[:, 12 * q:12 * q + 12].reshape(48, 128, 1024)),
            bmodsh=np.ascontiguousarray(np.repeat(bmodv[:, :, 12 * q:12 * q + 12].reshape(128, 48), 3, axis=1)), g2=np.ascontiguousarray(g2),
            wup=wup, wdn=wdn, cw=cw, fwin=fwin, fwo=fwo, gwo=gwo, gwin=gwin,
            gwins=np.ascontiguousarray(gwin.reshape(2, 4, 128, 6400)[:, q]),
            wg2=wg2f, wg2s=np.ascontiguousarray(wg2[:, q]),
            gng=np.ascontiguousarray(gng), st0=np.ascontiguousarray(state_gla[b, :, :, q].reshape(4, 128, 256)),
            cf32=cf32, cbf=cbf, posL=posL, sel=sel))
    if 'nc' not in _NC_CACHE:
        _NC_CACHE['nc'] = build_program()
    res = run_bass_kernel_spmd(_NC_CACHE['nc'], in_maps, core_ids=list(range(8)))
    y_prompt = np.zeros((16, 256, 1024), np.float32)
    y_sample = np.zeros((2, 2048, 1024), np.float32)
    new_state = np.zeros((16, 2, 2, 4, 128, 256), np.float32)
    for i in range(8):
        b, q = i // 4, i % 4
        r = res.results[i]
        Y = np.asarray(r["yT"], dtype=np.float32).reshape(128, 8, NT).transpose(2, 1, 0).reshape(NT, 1024)
        y_sample[b, q * 512:(q + 1) * 512] = Y[0:512]
        y_prompt[2 * i] = Y[512:768]
        y_prompt[2 * i + 1] = Y[768:1024]
        ns = np.asarray(r["ns"], dtype=np.float32).reshape(2, 2, 2, 4, 128, 256)
        new_state[2 * i] = ns[0]
        new_state[2 * i + 1] = ns[1]
    return (y_prompt, y_sample, new_state)
```

```python
import os
import numpy as np
import ml_dtypes
from contextlib import ExitStack
import concourse.bass as bass
import concourse.mybir as mybir
from concourse.bass_utils import run_bass_kernel_spmd

F32 = mybir.dt.float32
BF16 = mybir.dt.bfloat16
AF = mybir.ActivationFunctionType
ALU = mybir.AluOpType

D = 1024
DEPTH = 4
FFN = 2816
NJ = 22
EPS = 1e-6
NT = 1024
ENGS = ['pe', 'act', 'dve', 'pool', 'sp']
CH = 2000
FLAGS = set(os.environ.get("KFLAGS", "gla,fnet,ffn").split(","))
NLAYERS = int(os.environ.get("KLAYERS", "4"))


class Ins:
    __slots__ = ('eng', 'fn', 'deps', 'sig', 'kind', 'inc', 'ms', 'sem', 'val', 'n', 'prevwait')


class Sched:
    def __init__(self):
        self.L = {e: [] for e in ENGS}
        self.recs = {}

    def add(self, eng, fn, reads=(), writes=(), kind='c', inc=1):
        lst = self.L[eng]
        idx = len(lst)
        ins = Ins()
        ins.eng = eng; ins.fn = fn; ins.kind = kind; ins.inc = inc
        ins.sig = (kind != 'c')
        deps = set()
        inorder = (kind == 'c')
        psr = [(sp, lo // 2048 * 2048, (hi + 2047) // 2048 * 2048) for (sp, lo, hi) in reads if sp == 'ps']
        psw = [(sp, lo // 2048 * 2048, (hi + 2047) // 2048 * 2048) for (sp, lo, hi) in writes if sp == 'ps']
        reads = [r for r in reads if r[0] != 'ps']
        writes = [w for w in writes if w[0] != 'ps'] + psr + psw
        for (sp, lo, hi) in reads:
            rl = self.recs.setdefault(sp, [])
            for r in rl:
                if r[2] == 'W' and r[0] < hi and lo < r[1]:
                    deps.add((r[3], r[4]))
        for (sp, lo, hi) in writes:
            rl = self.recs.setdefault(sp, [])
            keep = []
            for r in rl:
                if r[0] < hi and lo < r[1]:
                    deps.add((r[3], r[4]))
                    if r[0] >= lo and r[1] <= hi:
                        continue
                keep.append(r)
            keep.append((lo, hi, 'W', eng, idx))
            self.recs[sp] = keep
        for (sp, lo, hi) in reads:
            rl = self.recs[sp]
            if inorder:
                rl = [r for r in rl if not (r[2] == 'R' and r[3] == eng and r[0] >= lo and r[1] <= hi)]
            rl.append((lo, hi, 'R', eng, idx))
            self.recs[sp] = rl
        if eng == 'pe':
            deps = {d for d in deps if d[0] != 'pe'}
        deps.discard((eng, idx))
        ins.deps = deps
        lst.append(ins)
        return ins

    def finalize(self, nc, stack):
        for e in ENGS:
            for ins in self.L[e]:
                for (de, di) in ins.deps:
                    self.L[de][di].sig = True
        self.bank = {}
        self.pool = {}
        NPOOL = 8
        for e in ENGS:
            ms = 0
            na = 0
            for ins in self.L[e]:
                if ins.kind == 'c':
                    if ins.sig:
                        ins.ms = ms
                        ms += 1
                elif ins.kind == 'd':
                    ins.n = na
                    na += 1
            nb = (ms + CH - 1) // CH
            self.bank[e] = [stack.enter_context(nc.semaphore("b_%s_%d" % (e, i))) for i in range(nb)]
            if na:
                self.pool[e] = [stack.enter_context(nc.semaphore("a_%s_%d" % (e, i))) for i in range(min(NPOOL, na))]
            ncc = 0
            for ins in self.L[e]:
                if ins.kind == 'c':
                    if ins.sig:
                        ins.sem = self.bank[e][ins.ms // CH]
                        ins.val = ins.ms % CH + 1
                elif ins.kind == 'cc':
                    ins.sem = stack.enter_context(nc.semaphore("cc_%s_%d" % (e, ncc)))
                    ncc += 1
                else:
                    P = len(self.pool[e])
                    ins.sem = self.pool[e][ins.n % P]
                    ins.val = ins.inc * 0
        for e in ENGS:
            if e not in self.pool:
                continue
            cum = {}
            prev = {}
            lastcc = None
            for ins in self.L[e]:
                if ins.kind == 'cc':
                    ins.prevwait = lastcc
                    ins.val = 1
                    lastcc = (ins.sem, 1)
                    continue
                if ins.kind != 'c':
                    k = id(ins.sem)
                    ins.prevwait = prev.get(k)
                    cum[k] = cum.get(k, 0) + ins.inc
                    ins.val = cum[k]
                    prev[k] = (ins.sem, ins.val)

    def emit(self, eng, engobj):
        waited_ms = {e: -1 for e in ENGS}
        waited_async = {}
        for ins in self.L[eng]:
            waits = []
            if ins.kind != 'c' and getattr(ins, 'prevwait', None) is not None:
                waits.append(('a', ins.prevwait[0], ins.prevwait[1]))
            for (de, di) in sorted(ins.deps, key=lambda t: (t[0], t[1])):
                d = self.L[de][di]
                if d.kind == 'c':
                    if d.ms > waited_ms[de]:
                        waits.append(('c', de, d.ms))
                else:
                    waits.append(('a', d.sem, d.val))
            best = {}
            for w in waits:
                if w[0] == 'c':
                    best[w[1]] = max(best.get(w[1], -1), w[2])
            for de, m in best.items():
                engobj.wait_ge(self.bank[de][m // CH], m % CH + 1)
                waited_ms[de] = m
            for w in waits:
                if w[0] == 'a':
                    k = id(w[1])
                    if waited_async.get(k, 0) < w[2]:
                        engobj.wait_ge(w[1], w[2])
                        waited_async[k] = w[2]
            if ins.fn is not None:
                h = ins.fn(engobj)
                if ins.sig:
                    h.then_inc(ins.sem, ins.inc)


class View:
    def __init__(self, arena, off, shape, dt):
        self.off = off
        self.shape = tuple(shape)
        self.dt = dt
        self.esz = 2 if dt == BF16 else 4
        n = int(np.prod(shape))
        self.n = n
        ap = arena[:, off // 2: off // 2 + n * self.esz // 2]
        if dt != BF16:
            ap = ap.bitcast(dt)
        if len(shape) == 2:
            ap = ap.rearrange('p (a b) -> p a b', a=shape[0])
        elif len(shape) == 3:
            ap = ap.rearrange('p (a b c) -> p a b c', a=shape[0], b=shape[1])
        self.ap = ap
        st = []
        s = 1
        for d_ in reversed(self.shape):
            st.append(s)
            s *= d_
        self.strides = tuple(reversed(st))

    def __getitem__(self, idx):
        return self.ap[idx]

    def reg(self, *idx):
        lo = 0
        hi = 0
        for i, d_ in enumerate(self.shape):
            if i < len(idx) and idx[i] is not None:
                it = idx[i]
                if isinstance(it, int):
                    a, b = it, it + 1
                else:
                    a, b = it
            else:
                a, b = 0, d_
            lo += a * self.strides[i]
            hi += (b - 1) * self.strides[i]
        hi += 1
        return ('sb', self.off + lo * self.esz, self.off + hi * self.esz)


class Arena:
    def __init__(self, ap, nbytes):
        self.ap = ap
        self.nbytes = nbytes
        self.top = 0

    def alloc(self, shape, dt):
        esz = 2 if dt == BF16 else 4
        n = int(np.prod(shape)) * esz
        off = (self.top + 31) // 32 * 32
        assert off + n <= self.nbytes, "arena overflow %d > %d" % (off + n, self.nbytes)
        self.top = off + n
        self.peak = max(getattr(self, 'peak', 0), self.top)
        return View(self.ap, off, shape, dt)

    def mark(self):
        return self.top

    def release(self, m):
        self.top = m


ARENA_BYTES = 206 * 1024
NSLOT = 12


def build_program():
    nc = bass.Bass("TRN2", target_bir_lowering=False)
    S = Sched()

    def din(name, shape, dt=F32):
        return nc.dram_tensor(name, list(shape), dt, kind="ExternalInput")

    d_xT = din("xT", [128, 8 * NT])
    d_cT = din("cT", [128, 24])
    d_wmod = din("wmodsh", [DEPTH * 12, 128, 1024])
    d_bmod = din("bmodsh", [128, 144])
    d_g2 = din("g2", [128, 9 * 16])
    d_wup = din("wup", [DEPTH * NJ, 128, 2048])
    d_wdn = din("wdn", [DEPTH * NJ, 128, 1024])
    d_cw = din("cw", [128, DEPTH * NJ * 4])
    d_fwin = din("fwin", [2 * 8, 128, 1024])
    d_fwo = din("fwo", [2 * 8, 128, 1024])
    d_gwo = din("gwo", [2 * 8, 128, 1024])
    d_gwin = din("gwin", [2 * 4, 128, 6400])
    d_gwins = din("gwins", [2, 128, 6400])
    d_wg2 = din("wg2", [2 * 4, 33, 256])
    d_wg2s = din("wg2s", [2, 33, 256])
    d_gng = din("gng", [128, 4])
    d_st0 = din("st0", [4, 128, 256])
    d_cf32 = din("cf32", [128, 4 * 128 + 4])
    d_cbf = din("cbf", [128, 3 * 128 + 4 * 512], BF16)
    d_posL = din("posL", [128, 2 * 16 * 512], BF16)
    d_sel = din("sel", [128, 14])
    d_yT = nc.dram_tensor("yT", [128, 8 * NT], F32, kind="ExternalOutput")
    d_ns = nc.dram_tensor("ns", [2 * 2 * 2 * 4, 128, 256], F32, kind="ExternalOutput")
    ag_h_in = nc.dram_tensor("ag_h_in", [1024, 512], BF16)
    ag_h_out = nc.dram_tensor("ag_h_out", [4096, 512], BF16)
    ag_o_in = nc.dram_tensor("ag_o_in", [256, 2048], BF16)
    ag_o_out = nc.dram_tensor("ag_o_out", [1024, 2048], BF16)
    ag_u_in = [nc.dram_tensor("ag_u%d_in" % i, [512, 1024], BF16) for i in range(2)]
    ag_u_out = [nc.dram_tensor("ag_u%d_out" % i, [2048, 1024], BF16) for i in range(2)]
    ag_x_in = nc.dram_tensor("ag_x_in", [1024, 128], BF16)
    ag_x_out = nc.dram_tensor("ag_x_out", [4096, 128], BF16)
    ag_m_in = [nc.dram_tensor("ag_m0_in", [128, 36], F32), nc.dram_tensor("ag_m1_in", [128, 108], F32)]
    ag_m_out = [nc.dram_tensor("ag_m0_out", [512, 36], F32), nc.dram_tensor("ag_m1_out", [512, 108], F32)]
    RG = [[0, 1, 2, 3], [4, 5, 6, 7]]

    stack = ExitStack()
    arena_t = stack.enter_context(nc.sbuf_tensor("arena", [128, ARENA_BYTES // 2], BF16))
    ps_t = stack.enter_context(nc.psum_tensor("ps", [128, 4096], F32))
    AR = Arena(arena_t, ARENA_BYTES)

    ps_state = {'p': 0}

    def psalloc(ncols):
        nrot = ps_state.get('nrot', 8)
        p = ps_state['p']
        if p >= nrot:
            p = 0
        ps_state['p'] = (p + 1) % nrot
        lo = p * 512
        return lo, ('ps', lo * 4, (lo + ncols) * 4)

    def PS(lo, ncols, parts=128):
        return ps_t[0:parts, lo:lo + ncols]

    def mm(out, oreg, lhsT, lreg, rhs, rreg, start, stop):
        S.add('pe', lambda e: e.matmul(out, lhsT, rhs, start=start, stop=stop), reads=[lreg, rreg], writes=[oreg])

    def act(out, in_, func, reads, writes, bias=None, scale=None):
        kw = {}
        if bias is not None:
            kw['bias'] = bias
        if scale is not None:
            kw['scale'] = scale
        S.add('act', lambda e: e.activation(out, in_, func, **kw), reads=reads, writes=writes)

    def ts(out, in0, s1, s2, op0, op1, reads, writes):
        if op1 is None:
            S.add('dve', lambda e: e.tensor_scalar(out, in0, s1, None, op0), reads=reads, writes=writes)
        else:
            S.add('dve', lambda e: e.tensor_scalar(out, in0, s1, s2, op0, op1), reads=reads, writes=writes)

    def stt(out, in0, sc, in1, op0, op1, reads, writes):
        S.add('dve', lambda e: e.scalar_tensor_tensor(out, in0, sc, in1, op0, op1), reads=reads, writes=writes)

    def tt(out, in0, in1, op, reads, writes):
        S.add('dve', lambda e: e.tensor_tensor(out, in0, in1, op), reads=reads, writes=writes)

    def dma(q, out, in_, reads, writes):
        S.add(q, lambda e: e.dma_start(out=out, in_=in_), reads=reads, writes=writes, kind='d', inc=16)

    def allgather(in_t, out_t, groups=None):
        groups = groups or RG
        S.add('pool', lambda e: e.collective_compute("AllGather", ALU.bypass, replica_groups=groups,
                                                      ins=[in_t.ap().opt()], outs=[out_t.ap().opt()]),
              reads=[('d:' + in_t.name, 0, 1)], writes=[('d:' + out_t.name, 0, 1)], kind='cc', inc=1)

    def dreg(t):
        return ('d:' + t.name, 0, 1)

    x = AR.alloc([8, NT], F32)
    h = AR.alloc([8, NT], BF16)
    ring = AR.alloc([NSLOT, 1024], BF16)
    cf32 = AR.alloc([4 * 128 + 4], F32)
    cbf = AR.alloc([3 * 128 + 4 * 512], BF16)
    sel = AR.alloc([14], F32)
    cT = AR.alloc([24], F32)
    scb = AR.alloc([8, 3], BF16)
    bmod = AR.alloc([144], F32)
    msh = AR.alloc([144], F32)
    mall = AR.alloc([4, 144], F32)
    g2 = AR.alloc([9 * 16], F32)
    cw = AR.alloc([DEPTH * NJ * 4], F32)
    gng = AR.alloc([4], F32)
    modv = AR.alloc([DEPTH, 96], F32)
    A1 = AR.alloc([DEPTH, 16], F32)
    A2 = AR.alloc([DEPTH, 16], F32)

    ring_state = {'p': 0}
    ns_keys = []
    y_keys = []

    def walloc(nelem):
        ns = (nelem + 1023) // 1024
        p = ring_state['p']
        if p + ns > NSLOT:
            p = 0
        ring_state['p'] = p + ns
        off = ring.off + p * 2048
        return off

    def wload(dram_ap, shape):
        n = int(np.prod(shape))
        off = walloc(n)
        v = View(arena_t, off, shape, BF16)
        dma('pool', v.ap, dram_ap, reads=[], writes=[('sb', off, off + n * 2)])
        return v

    def cf(i):
        return cf32.ap[:, i * 128:(i + 1) * 128], cf32.reg((i * 128, (i + 1) * 128))
    one_col = cf32.ap[:, 512:513]
    one_col_reg = cf32.reg((512, 516))
    eps_col = cf32.ap[:, 513:514]
    maskF = cbf.ap[:, 0:128]; maskB = cbf.ap[:, 128:256]; ones_bf = cbf.ap[:, 256:384]
    cbf_reg = cbf.reg()
    chC = cbf.ap[:, 384:896].rearrange('p (a b) -> p a b', a=2)
    chS = cbf.ap[:, 896:1408].rearrange('p (a b) -> p a b', a=2)
    pC = cbf.ap[:, 1408:1920].rearrange('p (a b) -> p a b', a=2)
    pSn = cbf.ap[:, 1920:2432].rearrange('p (a b) -> p a b', a=2)

    dma('sp', cT.ap, d_cT[:, :], [], [cT.reg()])
    dma('sp', bmod.ap, d_bmod[:, :], [], [bmod.reg()])
    dma('sp', sel.ap, d_sel[:, :], [], [sel.reg()])
    dma('sp', g2.ap, d_g2[:, :], [], [g2.reg()])
    dma('sp', cf32.ap, d_cf32[:, :], [], [cf32.reg()])
    dma('sp', cbf.ap, d_cbf[:, :], [], [cbf.reg()])
    dma('sp', cw.ap, d_cw[:, :], [], [cw.reg()])
    dma('sp', gng.ap, d_gng[:, :], [], [gng.reg()])
    for k in range(8):
        dma('sp', x[:, k, :], d_xT[:, k * NT:(k + 1) * NT], [], [x.reg(k)])
    act(scb.ap.rearrange('p a b -> p (a b)'), cT.ap, AF.Silu, [cT.reg()], [scb.reg()])

    def mod_cols(part):
        layers = [0] if part == 0 else [1, 2, 3]
        return layers, layers[0] * 36, len(layers) * 36

    def mod_block(i):
        lo, preg = psalloc(128)
        w = wload(d_wmod[i, :, :], [1024])
        for k in range(8):
            mm(PS(lo, 3), preg, w.ap[:, k * 128:(k + 1) * 128], w.reg((k * 128, (k + 1) * 128)),
               scb[:, k, :], scb.reg(k), k == 0, k == 7)
        tt(msh.ap[:, i * 3:(i + 1) * 3], PS(lo, 3), bmod.ap[:, i * 3:(i + 1) * 3], ALU.add, [preg, bmod.reg()], [msh.reg((i * 3, (i + 1) * 3))])

    def mod_launch(part):
        layers, c0, ncol = mod_cols(part)
        dma('sp', ag_m_in[part][:, :], msh.ap[:, c0:c0 + ncol], [msh.reg((c0, c0 + ncol))], [dreg(ag_m_in[part])])
        allgather(ag_m_in[part], ag_m_out[part])

    def mod_finish(part):
        layers, c0, ncol = mod_cols(part)
        dma('sp', mall.ap[:, :, c0:c0 + ncol], ag_m_out[part][:, :].rearrange('(r p) n -> p r n', p=128),
            [dreg(ag_m_out[part])], [mall.reg(None, (c0, c0 + ncol))])
        for l in layers:
            src = mall.ap[:, :, l * 36:(l + 1) * 36].rearrange('p r (c v) -> p r c v', v=3)
            dst = modv[:, l, :].rearrange('p (r c v) -> p r c v', r=4, c=12)
            mreg = mall.reg(None, (l * 36, (l + 1) * 36))
            ts(dst[:, :, :, 0], src[:, :, :, 0], sel.ap[:, 12:13], None, ALU.mult, None, [mreg, sel.reg()], [modv.reg(l)])
            stt(dst[:, :, :, 0], src[:, :, :, 1], sel.ap[:, 13:14], dst[:, :, :, 0], ALU.mult, ALU.add,
                [mreg, sel.reg(), modv.reg(l)], [modv.reg(l)])
            S.add('dve', lambda e, o=dst[:, :, :, 1], i=src[:, :, :, 2]: e.tensor_copy(o, i), [mreg], [modv.reg(l)])
            stt(A1[:, l, :], modv[:, l, 16:32], 1.0, g2.ap[:, l * 16:(l + 1) * 16], ALU.add, ALU.mult,
                [modv.reg(l), g2.reg()], [A1.reg(l)])
            stt(A2[:, l, :], modv[:, l, 64:80], 1.0, g2.ap[:, (4 + l) * 16:(5 + l) * 16], ALU.add, ALU.mult,
                [modv.reg(l), g2.reg()], [A2.reg(l)])

    def mcol(l, kind, k, v):
        c = kind * 16 + k * 2 + v
        return modv[:, l, c:c + 1]

    def norm_tile(t0, v, Acol, Bcol, Areg, Breg, out_fn):
        nm = AR.mark()
        rstd = AR.alloc([512], F32)
        sq = AR.alloc([8, 512], BF16)
        ntmp = AR.alloc([2, 512], F32)
        act(sq.ap, x[:, :, t0:t0 + 512], AF.Square, [x.reg(None, (t0, t0 + 512))], [sq.reg()])
        lo, preg = psalloc(512)
        for k in range(8):
            mm(PS(lo, 512), preg, ones_bf, cbf_reg, sq[:, k, :], sq.reg(k), k == 0, k == 7)
        act(rstd.ap, PS(lo, 512), AF.Ln, [preg, one_col_reg], [rstd.reg()], bias=eps_col, scale=1.0 / D)
        act(rstd.ap, rstd.ap, AF.Exp, [rstd.reg()], [rstd.reg()], scale=-0.5)
        for k in range(8):
            tb = ntmp[:, k % 2, :]
            treg = ntmp.reg(k % 2)
            stt(tb, x[:, k, t0:t0 + 512], Acol(k), rstd.ap, ALU.mult, ALU.mult,
                [x.reg(k, (t0, t0 + 512)), rstd.reg(), Areg], [treg])
            out_fn(k, tb, treg)
        AR.release(nm)

    def norm_to_h(l, which):
        Av = A1 if which == 1 else A2
        shk = 0 if which == 1 else 3
        nm = AR.mark()
        tiles = ((0, 0), (1, 512))
        rst = [AR.alloc([512], F32) for _ in tiles]
        sqs = [AR.alloc([8, 512], BF16) for _ in tiles]
        tmp = [AR.alloc([2, 512], F32) for _ in tiles]
        pss = []
        for ti, (v, t0) in enumerate(tiles):
            act(sqs[ti].ap, x[:, :, t0:t0 + 512], AF.Square, [x.reg(None, (t0, t0 + 512))], [sqs[ti].reg()])
        for ti, (v, t0) in enumerate(tiles):
            lo, preg = psalloc(512)
            pss.append((lo, preg))
            for k in range(8):
                mm(PS(lo, 512), preg, ones_bf, cbf_reg, sqs[ti][:, k, :], sqs[ti].reg(k), k == 0, k == 7)
        for ti, (v, t0) in enumerate(tiles):
            lo, preg = pss[ti]
            act(rst[ti].ap, PS(lo, 512), AF.Ln, [preg, one_col_reg], [rst[ti].reg()], bias=eps_col, scale=1.0 / D)
        for ti, (v, t0) in enumerate(tiles):
            act(rst[ti].ap, rst[ti].ap, AF.Exp, [rst[ti].reg()], [rst[ti].reg()], scale=-0.5)
        for k in range(8):
            for ti, (v, t0) in enumerate(tiles):
                tb = tmp[ti][:, k % 2, :]
                treg = tmp[ti].reg(k % 2)
                stt(tb, x[:, k, t0:t0 + 512], Av[:, l, k * 2 + v:k * 2 + v + 1], rst[ti].ap, ALU.mult, ALU.mult,
                    [x.reg(k, (t0, t0 + 512)), rst[ti].reg(), Av.reg(l)], [treg])
                act(h[:, k, t0:t0 + 512], tb, AF.Identity, [treg, modv.reg(l)], [h.reg(k, (t0, t0 + 512))],
                    bias=mcol(l, shk, k, v))
        AR.release(nm)

    def final_norm_tile(ti):
        v, t0 = ((0, 0), (1, 512))[ti]

        def outf(k, tb, treg):
            key = ('d:yT%d' % len(y_keys), 0, 1)
            y_keys.append(key)
            dma('sp', d_yT[:, k * NT + t0: k * NT + t0 + 512], tb, [treg], [key])
        norm_tile(t0, v, lambda k: g2.ap[:, 8 * 16 + k * 2: 8 * 16 + k * 2 + 1], None, g2.reg(), None, outf)

    def resid_update(l, gkind, dout, v, t0, plo, preg):
        stt(x[:, dout, t0:t0 + 512], PS(plo, 512), mcol(l, gkind, dout, v), x[:, dout, t0:t0 + 512],
            ALU.mult, ALU.add, [preg, modv.reg(l), x.reg(dout, (t0, t0 + 512))], [x.reg(dout, (t0, t0 + 512))])

    def ffn_layer(l, final=False):
        m = AR.mark()
        norm_to_h(l, 2)
        asS = [AR.alloc([528], F32) for _ in range(2)]
        asP = [AR.alloc([520], F32) for _ in range(2)]
        cb = [AR.alloc([1024], F32) for _ in range(2)]
        J = 6
        actb = [AR.alloc([J, 1024], BF16) for _ in range(2)]
        NF = 12
        fring = AR.alloc([NF, 1024], BF16)
        fst = {'p': 0}

        def fload(dram_ap):
            p = fst['p']
            fst['p'] = (p + 1) % NF
            off = fring.off + p * 2048
            v = View(arena_t, off, [1024], BF16)
            dma('pool', v.ap, dram_ap, reads=[], writes=[('sb', off, off + 2048)])
            return v
        for b_ in asP + asS:
            S.add('dve', lambda e, b_=b_: e.memset(b_.ap, 0.0), [], [b_.reg()])
        groups = [list(range(0, 6)), list(range(6, 12)), list(range(12, 17)), list(range(17, 22))]
        for gi, grp in enumerate(groups):
            ab = actb[gi % 2]
            wds = []
            for jj, j in enumerate(grp):
                wu = wload(d_wup[l * NJ + j, :, :].rearrange('p (k n) -> p k n', k=8), [8, 256])
                wd = fload(d_wdn[l * NJ + j, :, :])
                wds.append(wd)
                aS = asS[j % 2]; aP = asP[j % 2]; c_ = cb[j % 2]
                cwb = (l * NJ + j) * 4
                w0 = cw.ap[:, cwb:cwb + 1]; w1 = cw.ap[:, cwb + 1:cwb + 2]; w2 = cw.ap[:, cwb + 2:cwb + 3]; bb = cw.ap[:, cwb + 3:cwb + 4]
                loA, rA = psalloc(512)
                for k in range(8):
                    mm(PS(loA, 512), rA, wu[:, k, 0:128], wu.reg(k, (0, 128)), h[:, k, 0:512], h.reg(k, (0, 512)), k == 0, k == 7)
                loAP, rAP = psalloc(512)
                for k in range(8):
                    mm(PS(loAP, 512), rAP, wu[:, k, 0:128], wu.reg(k, (0, 128)), h[:, k, 512:1024], h.reg(k, (512, 1024)), k == 0, k == 7)
                act(aS.ap[:, 1:521].rearrange('p (s t) -> p s t', s=8)[:, :, 0:64],
                    PS(loA, 512).rearrange('p (s t) -> p s t', s=8), AF.Copy, [rA], [aS.reg((1, 521))])
                act(c_.ap[:, 0:512], PS(loA, 512), AF.Identity, [rA, cw.reg()], [c_.reg((0, 512))], bias=bb, scale=w1)
                act(aP.ap[:, 1:515].rearrange('p (s t) -> p s t', s=2)[:, :, 0:256],
                    PS(loAP, 512).rearrange('p (s t) -> p s t', s=2), AF.Copy, [rAP], [aP.reg((1, 515))])
                act(c_.ap[:, 512:1024], PS(loAP, 512), AF.Identity, [rAP, cw.reg()], [c_.reg((512, 1024))], bias=bb, scale=w1)
                loG, rG = psalloc(512)
                for k in range(8):
                    mm(PS(loG, 512), rG, wu[:, k, 128:256], wu.reg(k, (128, 256)), h[:, k, 0:512], h.reg(k, (0, 512)), k == 0, k == 7)
                loGP, rGP = psalloc(512)
                for k in range(8):
                    mm(PS(loGP, 512), rGP, wu[:, k, 128:256], wu.reg(k, (128, 256)), h[:, k, 512:1024], h.reg(k, (512, 1024)), k == 0, k == 7)
                cS = c_.ap[:, 0:512].rearrange('p (s t) -> p s t', s=8)

                def sv(o):
                    return aS.ap[:, o:o + 520].rearrange('p (s t) -> p s t', s=8)[:, :, 0:64]
                stt(cS, sv(0), w0, cS, ALU.mult, ALU.add, [aS.reg(), cw.reg(), c_.reg((0, 512))], [c_.reg((0, 512))])
                stt(cS, sv(2), w2, cS, ALU.mult, ALU.add, [aS.reg(), cw.reg(), c_.reg((0, 512))], [c_.reg((0, 512))])
                cP = c_.ap[:, 512:1024].rearrange('p (s t) -> p s t', s=2)

                def pv(o):
                    return aP.ap[:, o:o + 514].rearrange('p (s t) -> p s t', s=2)[:, :, 0:256]
                stt(cP, pv(0), w0, cP, ALU.mult, ALU.add, [aP.reg(), cw.reg(), c_.reg((512, 1024))], [c_.reg((512, 1024))])
                stt(cP, pv(2), w2, cP, ALU.mult, ALU.add, [aP.reg(), cw.reg(), c_.reg((512, 1024))], [c_.reg((512, 1024))])
                act(c_.ap, c_.ap, AF.Silu, [c_.reg()], [c_.reg()])
                tt(ab[:, jj, 0:512], c_.ap[:, 0:512], PS(loG, 512), ALU.mult, [c_.reg((0, 512)), rG], [ab.reg(jj, (0, 512))])
                tt(ab[:, jj, 512:1024], c_.ap[:, 512:1024], PS(loGP, 512), ALU.mult, [c_.reg((512, 1024)), rGP], [ab.reg(jj, (512, 1024))])
                if l == 0 and NLAYERS > 1 and j < 12:
                    for i_ in range(12 + 3 * j, 15 + 3 * j):
                        mod_block(i_)
                    if j == 11:
                        mod_launch(1)
            for ti, (v, t0) in enumerate(((0, 0), (1, 512))):
                for dout in range(8):
                    loY, rY = psalloc(512)
                    for jj in range(len(grp)):
                        wd = wds[jj]
                        mm(PS(loY, 512), rY, wd.ap[:, dout * 128:(dout + 1) * 128], wd.reg((dout * 128, (dout + 1) * 128)),
                           ab[:, jj, t0:t0 + 512], ab.reg(jj, (t0, t0 + 512)), jj == 0, jj == len(grp) - 1)
                    resid_update(l, 5, dout, v, t0, loY, rY)
                if final and gi == len(groups) - 1:
                    final_norm_tile(ti)
        AR.release(m)

    def fnet_layer(l):
        j = l // 2
        m = AR.mark()
        posL = AR.alloc([2, 16, 512], BF16)
        dma('sp', posL.ap.rearrange('p a b c -> p (a b c)'), d_posL[:, :], [], [posL.reg()])
        norm_to_h(l, 1)
        m1 = AR.mark()
        uT = AR.alloc([8, NT], BF16)
        ucsP = AR.alloc([4, 2048], BF16)
        wins = [wload(d_fwin[j * 8 + c, :, :].rearrange('p (k n) -> p k n', k=8), [8, 128]) for c in range(8)]

        def uproj(t0):
            for c in range(8):
                w = wins[c]
                lo, pr = psalloc(512)
                for k in range(8):
                    mm(PS(lo, 512), pr, w[:, k, :], w.reg(k), h[:, k, t0:t0 + 512], h.reg(k, (t0, t0 + 512)), k == 0, k == 7)
                act(uT[:, c, t0:t0 + 512], PS(lo, 512), AF.Copy, [pr], [uT.reg(c, (t0, t0 + 512))])

        def chan_dft(t0, dst):
            for tc in range(4):
                for gq in range(4):
                    for (mat, off) in ((chC, 0), (chS, 1024)):
                        lo, pr = psalloc(256)
                        for kk in range(2):
                            cols = (t0 + tc * 128, t0 + (tc + 1) * 128)
                            mm(PS(lo, 256), pr, uT[:, 2 * gq + kk, cols[0]:cols[1]], uT.reg(2 * gq + kk, cols),
                               mat[:, kk, :], cbf_reg, kk == 0, kk == 1)
                        o0 = off + gq * 256
                        if (gq + (off > 0)) % 2 == 0:
                            act(dst[:, tc, o0:o0 + 256], PS(lo, 256), AF.Copy, [pr], [dst.reg(tc, (o0, o0 + 256))])
                        else:
                            S.add('dve', lambda e, o=dst[:, tc, o0:o0 + 256], i=PS(lo, 256): e.tensor_copy(o, i),
                                  [pr], [dst.reg(tc, (o0, o0 + 256))])

        def wo_proj(wos, v, t0):
            for dout in range(8):
                w = wos[dout]
                lo, pr = psalloc(512)
                for k in range(8):
                    mm(PS(lo, 512), pr, w[:, k, :], w.reg(k), h[:, k, t0:t0 + 512], h.reg(k, (t0, t0 + 512)), k == 0, k == 7)
                resid_update(l, 2, dout, v, t0, lo, pr)

        uproj(0)
        dma('sp', ag_h_in[:, :].rearrange('(k p) t -> p k t', p=128), uT[:, :, 0:512], [uT.reg(None, (0, 512))], [dreg(ag_h_in)])
        allgather(ag_h_in, ag_h_out)
        uproj(512)
        chan_dft(512, ucsP)
        for s in range(2):
            for c in range(8):
                lo, pr = psalloc(256)
                n = 0
                for tcl in range(2):
                    tc = s * 2 + tcl
                    for (off, mat) in ((0, pC), (1024, pSn)):
                        mm(PS(lo, 256), pr, ucsP[:, tc, off + c * 128: off + (c + 1) * 128], ucsP.reg(tc, (off + c * 128, off + (c + 1) * 128)),
                           mat[:, tcl, :], cbf_reg, n == 0, n == 3)
                        n += 1
                t0 = 512 + s * 256
                act(h[:, c, t0:t0 + 256], PS(lo, 256), AF.Copy, [pr], [h.reg(c, (t0, t0 + 256))])
        wos = [wload(d_fwo[j * 8 + dout, :, :].rearrange('p (k n) -> p k n', k=8), [8, 128]) for dout in range(8)]
        wo_proj(wos, 1, 512)
        AR.release(m1)
        ufull = AR.alloc([2, 16, 1024], BF16)
        upc = [AR.alloc([8, 512], BF16) for _ in range(2)]
        ncp = 0
        for r in range(4):
            ub = upc[r % 2]
            dma('sp', ub.ap, ag_h_out[r * 1024:(r + 1) * 1024, :].rearrange('(k p) t -> p k t', p=128), [dreg(ag_h_out)], [ub.reg()])
            for tcl in range(4):
                tc = r * 4 + tcl
                for gq in range(4):
                    for (mat, pi) in ((chC, 0), (chS, 1)):
                        lo, pr = psalloc(256)
                        for kk in range(2):
                            mm(PS(lo, 256), pr, ub[:, 2 * gq + kk, tcl * 128:(tcl + 1) * 128], ub.reg(2 * gq + kk, (tcl * 128, (tcl + 1) * 128)),
                               mat[:, kk, :], cbf_reg, kk == 0, kk == 1)
                        dst_ap = ufull[:, pi, tc, gq * 256:(gq + 1) * 256]
                        dst_rg = ufull.reg(pi, tc, (gq * 256, (gq + 1) * 256))
                        if ncp % 2 == 0:
                            act(dst_ap, PS(lo, 256), AF.Copy, [pr], [dst_rg])
                        else:
                            S.add('dve', lambda e, o=dst_ap, i=PS(lo, 256): e.tensor_copy(o, i), [pr], [dst_rg])
                        ncp += 1
        for c in range(8):
            lo, pr = psalloc(512)
            n = 0
            for tc in range(16):
                for pi in range(2):
                    mm(PS(lo, 512), pr, ufull[:, pi, tc, c * 128:(c + 1) * 128], ufull.reg(pi, tc, (c * 128, (c + 1) * 128)),
                       posL[:, pi, tc, :], posL.reg(pi, tc), n == 0, n == 31)
                    n += 1
            act(h[:, c, 0:512], PS(lo, 512), AF.Copy, [pr], [h.reg(c, (0, 512))])
        wo_proj(wos, 0, 0)
        AR.release(m)

    def gla_head(l, hT, hreg_fn, seqs, W, wg, gcol, o_dst, sbf_alias=None, after_phase1=None):
        j = l // 2
        m = AR.mark()
        ntok = sum(s['nch'] for s in seqs) * 128
        nchT = ntok // 128
        glr = AR.alloc([512], F32)
        sr = AR.alloc([2, ntok], BF16)
        qt = [AR.alloc([ntok], BF16) for _ in range(2)]
        kt = [AR.alloc([ntok], BF16) for _ in range(2)]
        kh = [AR.alloc([nchT, 128], BF16) for _ in range(2)]
        vt = AR.alloc([nchT, 256], BF16)
        dec = AR.alloc([2, nchT], F32)
        if sbf_alias is None:
            Sbf = [AR.alloc([nchT, 256], BF16) for _ in range(2)]
        else:
            Sbf = [View(arena_t, sbf_alias.off + z * nchT * 512, [nchT, 256], BF16) for z in range(2)]
        SstAll = [[[AR.alloc([256], F32) for _ in range(2)] for _ in range(2)] for _ in seqs]
        Lb = [AR.alloc([2, 128], F32) for _ in range(2)]
        etmp = [AR.alloc([2, 128], F32) for _ in range(2)]
        eqk = [AR.alloc([4, 128], F32) for _ in range(2)]
        ekh = [AR.alloc([2, 128], F32) for _ in range(2)]
        att = [AR.alloc([2, 128], BF16) for _ in range(2)]
        osq = [AR.alloc([2, 128], BF16) for _ in range(2)]
        orst = [AR.alloc([128], F32) for _ in range(2)]
        otmp = [AR.alloc([2, 128], F32) for _ in range(2)]
        S.add('dve', lambda e: e.memset(glr.ap[32:33, :], 1.0), [], [glr.reg()])
        loc = 0
        for s in seqs:
            s['loc'] = loc
            loc += s['nch'] * 128
        items = []
        for s in seqs:
            nt_ = s['nch'] * 128
            for b0 in range(0, nt_, 512):
                bw = min(512, nt_ - b0)
                for c in range(b0 // 128, (b0 + bw) // 128):
                    items.append((s, b0, bw, c, c == b0 // 128))

        def rpass(s, b0, bw):
            c0, lc = s['c0'], s['loc']
            for dvc in range(2):
                lo, pr = psalloc(bw)
                for k in range(8):
                    mm(PS(lo, bw), pr, W[:, k, 512 + dvc * 128: 640 + dvc * 128], W.reg(k, (512 + dvc * 128, 640 + dvc * 128)),
                       hT(k, c0 + b0, c0 + b0 + bw), hreg_fn(k, c0 + b0, c0 + b0 + bw), k == 0, k == 7)
                act(sr[:, dvc, lc + b0: lc + b0 + bw], PS(lo, bw), AF.Silu, [pr], [sr.reg(dvc, (lc + b0, lc + b0 + bw))])

        glr_all = (ntok <= 512)

        def prologue(s, b0, bw):
            c0, lc = s['c0'], s['loc']
            d0 = (lc + b0) if glr_all else 0
            lo, pr = psalloc(bw)
            for k in range(8):
                mm(PS(lo, bw, 32), pr, W[:, k, 768:800], W.reg(k, (768, 800)), hT(k, c0 + b0, c0 + b0 + bw), hreg_fn(k, c0 + b0, c0 + b0 + bw), k == 0, k == 7)
            act(glr.ap[0:32, d0:d0 + bw], PS(lo, bw, 32), AF.Copy, [pr], [glr.reg((d0, d0 + bw))])

        QB, KB = 6 * 512, 7 * 512
        regQB = ('ps', QB * 4, (QB + 512) * 4)
        regKB = ('ps', KB * 4, (KB + 512) * 4)

        def blockP(s, b0, bw):
            c0 = s['c0']
            for k in range(8):
                mm(PS(QB, bw), regQB, W[:, k, 0:128], W.reg(k, (0, 128)), hT(k, c0 + b0, c0 + b0 + bw), hreg_fn(k, c0 + b0, c0 + b0 + bw), k == 0, k == 7)
            for k in range(8):
                mm(PS(KB, bw), regKB, W[:, k, 128:256], W.reg(k, (128, 256)), hT(k, c0 + b0, c0 + b0 + bw), hreg_fn(k, c0 + b0, c0 + b0 + bw), k == 0, k == 7)

        def stageP(s, b0, bw, c):
            c0 = s['c0']
            a0, a1 = c0 + c * 128, c0 + (c + 1) * 128
            loKV, prKV = psalloc(384)
            for k in range(8):
                mm(PS(loKV, 384), prKV, hT(k, a0, a1), hreg_fn(k, a0, a1), W[:, k, 128:512], W.reg(k, (128, 512)), k == 0, k == 7)
            return (loKV, prKV)

        def stageA(s, b0, bw, c):
            lc = s['loc']
            ci = (lc // 128) + c
            g0, g1 = c * 128 - b0, (c + 1) * 128 - b0
            if glr_all:
                g0, g1 = lc + c * 128, lc + (c + 1) * 128
            pb = ci % 2
            loL, prL = psalloc(256)
            for z in range(2):
                mm(PS(loL + z * 128, 128), prL, glr.ap[0:33, g0:g1], glr.reg((g0, g1)), wg.ap[0:33, z * 128:(z + 1) * 128], wg.reg(), True, True)
            act(etmp[pb].ap, PS(loL, 256).rearrange('p (a b) -> p a b', a=2), AF.Exp, [prL], [etmp[pb].reg()], scale=-1.0)
            act(Lb[pb].ap, etmp[pb].ap, AF.Ln, [etmp[pb].reg(), one_col_reg], [Lb[pb].reg()], bias=one_col)
            loC, prC = psalloc(512)
            for z in range(2):
                tri, trireg = cf(z)
                mm(PS(loC + z * 128, 128), prC, Lb[pb][:, z, :], Lb[pb].reg(z), tri, trireg, True, True)
            for z in range(2):
                tri2, tri2reg = cf(2 + z)
                mm(PS(loC + 256 + z * 128, 128), prC, tri2, tri2reg, Lb[pb][:, z, :], Lb[pb].reg(z), True, True)
            return (loC, prC)

        def stageB(s, b0, bw, c, ctx, pctx):
            loC, prC = ctx
            loKV, prKV = pctx
            qo = c * 128 - b0
            c0, lc = s['c0'], s['loc']
            ci = (lc // 128) + c
            l0, l1 = lc + c * 128, lc + (c + 1) * 128
            pb = ci % 2
            cview = PS(loC, 256).rearrange('p (a b) -> p a b', a=2)
            act(eqk[pb][:, 0:2, :], cview, AF.Exp, [prC], [eqk[pb].reg((0, 2))], scale=-1.0 / 16)
            act(eqk[pb][:, 2:4, :], cview, AF.Exp, [prC], [eqk[pb].reg((2, 4))], scale=1.0 / 16)
            act(ekh[pb].ap, PS(loC + 256, 256).rearrange('p (a b) -> p a b', a=2), AF.Exp, [prC], [ekh[pb].reg()], scale=-1.0 / 16)
            act(dec[:, :, ci], PS(loC + 127, 2), AF.Exp, [prC], [dec.reg(None, ci)], scale=-1.0 / 16)
            for z in range(2):
                stt(qt[z].ap[:, l0:l1], PS(QB + qo, 128), 128.0 ** -0.5, eqk[pb][:, z, :], ALU.mult, ALU.mult,
                    [regQB, eqk[pb].reg(z)], [qt[z].reg((l0, l1))])
                tt(kt[z].ap[:, l0:l1], PS(KB + qo, 128), eqk[pb][:, 2 + z, :], ALU.mult,
                   [regKB, eqk[pb].reg(2 + z)], [kt[z].reg((l0, l1))])
                tt(kh[z][:, ci, :], PS(loKV, 128), ekh[pb][:, z, :], ALU.mult, [prKV, ekh[pb].reg(z)], [kh[z].reg(ci)])
            act(vt[:, ci, :], PS(loKV + 128, 256), AF.Copy, [prKV], [vt.reg(ci)])

        for s in seqs:
            nt_ = s['nch'] * 128
            for b0 in range(0, nt_, 512):
                rpass(s, b0, min(512, nt_ - b0))
                if glr_all:
                    prologue(s, b0, min(512, nt_ - b0))
        ps_state['nrot'] = 6
        ctxs = {}
        pctxs = {}
        for idx, it in enumerate(items):
            if it[4]:
                if not glr_all:
                    prologue(it[0], it[1], it[2])
                blockP(it[0], it[1], it[2])
                pctxs[idx] = stageP(*it[:4])
                ctxs[idx] = stageA(*it[:4])
            nx = items[idx + 1] if idx + 1 < len(items) else None
            if nx is not None and not nx[4]:
                pctxs[idx + 1] = stageP(*nx[:4])
                ctxs[idx + 1] = stageA(*nx[:4])
            stageB(it[0], it[1], it[2], it[3], ctxs.pop(idx), pctxs.pop(idx))
        ps_state['nrot'] = 8
        if after_phase1 is not None:
            after_phase1()
        for si, s in enumerate(seqs):
            nch, lc = s['nch'], s['loc']
            cbase = lc // 128
            Sst = SstAll[si]
            cur = [None, None]
            for z in range(2):
                if s['init'] is not None:
                    dma('sp', Sst[z][0].ap, s['init'][z], [], [Sst[z][0].reg()])
                    cur[z] = 0
            for n_ in range(nch):
                for z in range(2):
                    c = n_ if z == 0 else nch - 1 - n_
                    ci = cbase + c
                    if cur[z] is not None:
                        st = Sst[z][cur[z]]
                        act(Sbf[z][:, ci, :], st.ap, AF.Copy, [st.reg()], [Sbf[z].reg(ci)])
                    else:
                        S.add('dve', lambda e, o=Sbf[z][:, ci, :]: e.memset(o, 0.0), [], [Sbf[z].reg(ci)])
                    last = (n_ == nch - 1)
                    if last and s['fin'] is None:
                        continue
                    lo, pr = psalloc(256)
                    mm(PS(lo, 256), pr, kh[z][:, ci, :], kh[z].reg(ci), vt[:, ci, :], vt.reg(ci), True, True)
                    if cur[z] is not None:
                        st = Sst[z][cur[z]]
                        nst = Sst[z][1 - cur[z]]
                        stt(nst.ap, st.ap, dec[:, z, ci:ci + 1], PS(lo, 256), ALU.mult, ALU.add,
                            [st.reg(), dec.reg(z, ci), pr], [nst.reg()])
                        cur[z] = 1 - cur[z]
                    else:
                        nst = Sst[z][0]
                        S.add('dve', lambda e, o=nst.ap, i=PS(lo, 256): e.tensor_copy(o, i), [pr], [nst.reg()])
                        cur[z] = 0
            if s['fin'] is not None:
                for z in range(2):
                    st = Sst[z][cur[z]]
                    key = ('d:ns%d' % len(ns_keys), 0, 1)
                    ns_keys.append(key)
                    dma('sp', s['fin'][z], st.ap, [st.reg()], [key])
        chunks = []
        for s in seqs:
            for c in range(s['nch']):
                chunks.append((s, c))

        def emit_att(s, c):
            lc = s['loc']
            ci = (lc // 128) + c
            l0, l1 = lc + c * 128, lc + (c + 1) * 128
            pb = ci % 2
            loA, prA = psalloc(256)
            for z in range(2):
                mm(PS(loA + z * 128, 128), prA, kt[z].ap[:, l0:l1], kt[z].reg((l0, l1)), qt[z].ap[:, l0:l1], qt[z].reg((l0, l1)), True, True)
            tt(att[pb].ap, PS(loA, 256).rearrange('p (a b) -> p a b', a=2), cbf.ap[:, 0:256].rearrange('p (a b) -> p a b', a=2),
               ALU.mult, [prA, cbf_reg], [att[pb].reg()])

        if chunks:
            emit_att(*chunks[0])
        for idx, (s, c) in enumerate(chunks):
            c0, lc = s['c0'], s['loc']
            ci = (lc // 128) + c
            l0, l1 = lc + c * 128, lc + (c + 1) * 128
            pb = ci % 2
            if idx + 1 < len(chunks):
                emit_att(*chunks[idx + 1])
            loO, prO = psalloc(256)
            for dvc in range(2):
                o_ps = PS(loO + dvc * 128, 128)
                mm(o_ps, prO, vt[:, ci, dvc * 128:(dvc + 1) * 128], vt.reg(ci), att[pb][:, 0, :], att[pb].reg(0), True, False)
                mm(o_ps, prO, vt[:, ci, dvc * 128:(dvc + 1) * 128], vt.reg(ci), att[pb][:, 1, :], att[pb].reg(1), False, False)
                mm(o_ps, prO, Sbf[0][:, ci, dvc * 128:(dvc + 1) * 128], Sbf[0].reg(ci), qt[0].ap[:, l0:l1], qt[0].reg((l0, l1)), False, False)
                mm(o_ps, prO, Sbf[1][:, ci, dvc * 128:(dvc + 1) * 128], Sbf[1].reg(ci), qt[1].ap[:, l0:l1], qt[1].reg((l0, l1)), False, True)
            oview = PS(loO, 256).rearrange('p (a b) -> p a b', a=2)
            act(osq[pb].ap, oview, AF.Square, [prO], [osq[pb].reg()])
            loS, prS = psalloc(128)
            for dvc in range(2):
                mm(PS(loS, 128), prS, ones_bf, cbf_reg, osq[pb][:, dvc, :], osq[pb].reg(dvc), dvc == 0, dvc == 1)
            act(orst[pb].ap, PS(loS, 128), AF.Ln, [prS, one_col_reg], [orst[pb].reg()], bias=eps_col, scale=1.0 / 256)
            act(orst[pb].ap, orst[pb].ap, AF.Exp, [orst[pb].reg()], [orst[pb].reg()], scale=-0.5)
            for dvc in range(2):
                stt(otmp[pb][:, dvc, :], PS(loO + dvc * 128, 128), gcol(dvc), orst[pb].ap, ALU.mult, ALU.mult,
                    [prO, orst[pb].reg(), gng.reg()], [otmp[pb].reg(dvc)])
                dap, dr = o_dst(c0 + c * 128, dvc)
                tt(dap, otmp[pb][:, dvc, :], sr[:, dvc, l0:l1], ALU.mult, [otmp[pb].reg(dvc), sr.reg(dvc, (l0, l1))], [dr])
        AR.release(m)

    def gla_layer(l):
        j = l // 2
        m = AR.mark()
        Wn = {}

        wgvs = [AR.alloc([256], F32) for _ in range(2)]
        npf = [0]

        def prefetch(key, dram, wgdram):
            Wn[key] = (wload(dram.rearrange('p (k n) -> p k n', k=8), [8, 800]), wgvs[npf[0] % 2])
            dma('sp', wgvs[npf[0] % 2].ap[0:33, :], wgdram, [], [wgvs[npf[0] % 2].reg()])
            npf[0] += 1
        prefetch(0, d_gwin[j * 4 + 0, :, :], d_wg2[j * 4 + 0, :, :])
        norm_to_h(l, 1)
        oS = AR.alloc([2, 2048], BF16)
        oall = AR.alloc([8, NT], BF16)
        hfull = AR.alloc([8, 2048], BF16)
        dma('sp', ag_h_in[:, :].rearrange('(k p) t -> p k t', p=128), h[:, :, 0:512], [h.reg(None, (0, 512))], [dreg(ag_h_in)])
        allgather(ag_h_in, ag_h_out)

        def run_prompt_head(hd, nxt):
            W, wgv = Wn[hd]
            seqs = []
            for s in range(2):
                fin = tuple(d_ns[((s * 2 + j) * 2 + z) * 4 + hd, :, :] for z in range(2))
                seqs.append(dict(c0=512 + s * 256, nch=2, init=None, fin=fin))
            gla_head(l, lambda k, a, b: h[:, k, a:b], lambda k, a, b: h.reg(k, (a, b)), seqs, W, wgv,
                     lambda dvc: gng.ap[:, j * 2 + dvc: j * 2 + dvc + 1],
                     lambda ca, dvc, hd=hd: (oall[:, hd * 2 + dvc, ca:ca + 128], oall.reg(hd * 2 + dvc, (ca, ca + 128))),
                     after_phase1=nxt)

        run_prompt_head(0, lambda: prefetch(1, d_gwin[j * 4 + 1, :, :], d_wg2[j * 4 + 1, :, :]))
        for r in range(4):
            dma('sp', hfull[:, :, r * 512:(r + 1) * 512], ag_h_out[r * 1024:(r + 1) * 1024, :].rearrange('(k p) t -> p k t', p=128),
                [dreg(ag_h_out)], [hfull.reg(None, (r * 512, (r + 1) * 512))])
        run_prompt_head(1, lambda: prefetch('s', d_gwins[j, :, :], d_wg2s[j, :, :]))
        init = (d_st0[j * 2 + 0, :, :], d_st0[j * 2 + 1, :, :])
        gla_head(l, lambda k, a, b: hfull[:, k, a:b], lambda k, a, b: hfull.reg(k, (a, b)),
                 [dict(c0=0, nch=16, init=init, fin=None)], Wn['s'][0], Wn['s'][1],
                 lambda dvc: gng.ap[:, j * 2 + dvc: j * 2 + dvc + 1],
                 lambda ca, dvc: (oS[:, dvc, ca:ca + 128], oS.reg(dvc, (ca, ca + 128))), sbf_alias=hfull,
                 after_phase1=lambda: prefetch(2, d_gwin[j * 4 + 2, :, :], d_wg2[j * 4 + 2, :, :]))
        dma('sp', ag_o_in[:, :].rearrange('(c p) t -> p c t', p=128), oS.ap, [oS.reg()], [dreg(ag_o_in)])
        allgather(ag_o_in, ag_o_out)
        run_prompt_head(2, lambda: prefetch(3, d_gwin[j * 4 + 3, :, :], d_wg2[j * 4 + 3, :, :]))
        oblk = View(arena_t, hfull.off, [8, 4, 512], BF16)
        dma('sp', oblk.ap, ag_o_out[:, :].rearrange('(c p) (qq t) -> p c qq t', p=128, qq=4), [dreg(ag_o_out)], [oblk.reg()])
        run_prompt_head(3, None)
        wos = [wload(d_gwo[j * 8 + dout, :, :].rearrange('p (k n) -> p k n', k=8), [8, 128]) for dout in range(8)]

        def wo_proj(v, t0):
            for dout in range(8):
                w = wos[dout]
                lo, pr = psalloc(512)
                for k in range(8):
                    mm(PS(lo, 512), pr, w[:, k, :], w.reg(k), oall[:, k, t0:t0 + 512], oall.reg(k, (t0, t0 + 512)), k == 0, k == 7)
                resid_update(l, 2, dout, v, t0, lo, pr)
        wo_proj(1, 512)
        for qq in range(4):
            for c in range(8):
                if qq == 0:
                    ts(oall[:, c, 0:512], oblk[:, c, qq, :], sel.ap[:, qq:qq + 1], None, ALU.mult, None,
                       [oblk.reg(c, qq), sel.reg()], [oall.reg(c, (0, 512))])
                else:
                    stt(oall[:, c, 0:512], oblk[:, c, qq, :], sel.ap[:, qq:qq + 1], oall[:, c, 0:512], ALU.mult, ALU.add,
                        [oblk.reg(c, qq), sel.reg(), oall.reg(c, (0, 512))], [oall.reg(c, (0, 512))])
        wo_proj(0, 0)
        AR.release(m)

    for i_ in range(12):
        mod_block(i_)
    mod_launch(0)
    mod_finish(0)
    for l in range(NLAYERS):
        if l % 2 == 0:
            if 'gla' in FLAGS:
                gla_layer(l)
        else:
            if 'fnet' in FLAGS:
                fnet_layer(l)
        if 'ffn' in FLAGS:
            ffn_layer(l, final=(l == NLAYERS - 1))
        if l == 0 and NLAYERS > 1:
            mod_finish(1)
    if not ('ffn' in FLAGS and NLAYERS >= 1):
        for ti in range(2):
            final_norm_tile(ti)
    S.add('sp', None, reads=y_keys + ns_keys, writes=[])

    S.finalize(nc, stack)
    with stack:
        with nc.Block() as block:
            @block.tensor
            def _(e):
                S.emit('pe', e)

            @block.scalar
            def _(e):
                S.emit('act', e)

            @block.vector
            def _(e):
                S.emit('dve', e)

            @block.gpsimd
            def _(e):
                S.emit('pool', e)

            @block.sync
            def _(e):
                S.emit('sp', e)
    return nc


def _fm(a):
    T = a.shape[0]
    return np.ascontiguousarray(a.reshape(T, 8, 128).transpose(2, 1, 0)).reshape(128, 8 * T)


def _vec(v):
    return np.ascontiguousarray(v.reshape(-1, 128).T)


def _wblk(w, ncol):
    K, N = w.shape
    nk = K // 128
    return np.ascontiguousarray(w.reshape(nk, 128, N // ncol, ncol).transpose(2, 1, 0, 3)).reshape(N // ncol, 128, nk * ncol)


def _consts(q):
    j = np.arange(128)[:, None]
    i = np.arange(128)[None, :]
    triF = (j <= i).astype(np.float32)
    triB = (j >= i).astype(np.float32)
    triF2 = (j > i).astype(np.float32)
    triB2 = (j < i).astype(np.float32)
    cf32 = np.concatenate([triF, triB, triF2, triB2, np.ones((128, 4), np.float32)], axis=1)
    cf32[:, 513] = EPS
    bf = ml_dtypes.bfloat16
    n = np.arange(256)
    ang = 2 * np.pi * np.outer(n, n) / 256.0
    C = np.cos(ang); Sn = np.sin(ang)

    def two(mat):
        return mat.reshape(2, 128, 256).transpose(1, 0, 2).reshape(128, 512)
    cbf = np.concatenate([triF, triB, np.ones((128, 128)), two(C), two(Sn), two(C / 256.0), two(-Sn / 256.0)], axis=1).astype(bf)
    t = np.arange(2048, dtype=np.float64)[:, None]
    tp = (q * 512 + np.arange(512, dtype=np.float64))[None, :]
    angL = 2 * np.pi * ((t * tp) % 2048) / 2048.0
    sc = 1.0 / np.sqrt(2048.0 * 256.0)
    CL = (np.cos(angL) * sc).reshape(16, 128, 512).transpose(1, 0, 2)
    SLn = (-np.sin(angL) * sc).reshape(16, 128, 512).transpose(1, 0, 2)
    posL = np.stack([CL, SLn], axis=1).reshape(128, 2 * 16 * 512).astype(bf)
    return cf32, np.ascontiguousarray(cbf), np.ascontiguousarray(posL)


_NC_CACHE = {}


def kernel(x_prompt, x_sample, state_gla, c, c_ctx, norm_mix_g, norm_ffn_g, w_mod, b_mod,
           gla_w_in, gla_w_g2, gla_b_g, gla_norm_g, gla_w_o, fnet_w_in, fnet_w_o,
           ffn_w_up, ffn_conv_w, ffn_conv_b, ffn_w_down, final_norm_g):
    f = lambda a: np.asarray(a, dtype=np.float32)
    x_prompt, x_sample, state_gla, c, c_ctx = map(f, (x_prompt, x_sample, state_gla, c, c_ctx))
    norm_mix_g, norm_ffn_g, w_mod, b_mod = map(f, (norm_mix_g, norm_ffn_g, w_mod, b_mod))
    gla_w_in, gla_w_g2, gla_b_g, gla_norm_g, gla_w_o = map(f, (gla_w_in, gla_w_g2, gla_b_g, gla_norm_g, gla_w_o))
    fnet_w_in, fnet_w_o, ffn_w_up, ffn_conv_w, ffn_conv_b, ffn_w_down, final_norm_g = map(
        f, (fnet_w_in, fnet_w_o, ffn_w_up, ffn_conv_w, ffn_conv_b, ffn_w_down, final_norm_g))

    wmod = np.stack([_wblk(w_mod[l], 128) for l in range(4)], axis=0)
    bmodv = np.stack([_vec(b_mod[l]) for l in range(4)], axis=1)
    cT = np.ascontiguousarray(np.stack([_vec(c[0]), _vec(c[1]), _vec(c_ctx)], axis=2).reshape(128, 24))
    gains = [norm_mix_g[l] for l in range(4)] + [norm_ffn_g[l] for l in range(4)] + [final_norm_g]
    g2 = np.concatenate([np.repeat(_vec(g), 2, axis=1) for g in gains], axis=1)
    wup = []
    for l in range(4):
        a = _wblk(ffn_w_up[l][:, :FFN], 128).reshape(NJ, 128, 8, 128)
        g = _wblk(ffn_w_up[l][:, FFN:], 128).reshape(NJ, 128, 8, 128)
        wup.append(np.concatenate([a, g], axis=3).reshape(NJ, 128, 2048))
    wup = np.ascontiguousarray(np.concatenate(wup, axis=0))
    wdn = np.ascontiguousarray(ffn_w_down.reshape(4 * NJ, 128, 1024))
    cw = np.concatenate([ffn_conv_w, ffn_conv_b[:, None, :]], axis=1)
    cw = np.ascontiguousarray(cw.reshape(4, 4, NJ, 128).transpose(3, 0, 2, 1)).reshape(128, 4 * NJ * 4)
    fwin = np.concatenate([_wblk(fnet_w_in[j], 128) for j in range(2)], axis=0)
    fwo = np.concatenate([_wblk(fnet_w_o[j], 128) for j in range(2)], axis=0)
    gwo = np.concatenate([_wblk(gla_w_o[j], 128) for j in range(2)], axis=0)
    gwin = np.zeros((2, 4, 128, 8, 800), np.float32)
    wg2 = np.zeros((2, 4, 33, 256), np.float32)
    for j in range(2):
        w = gla_w_in[j].reshape(8, 128, 3104).transpose(1, 0, 2)
        for hd in range(4):
            gwin[j, hd, :, :, 0:128] = w[:, :, hd * 128:(hd + 1) * 128]
            gwin[j, hd, :, :, 128:256] = w[:, :, 512 + hd * 128: 512 + (hd + 1) * 128]
            gwin[j, hd, :, :, 256:512] = w[:, :, 1024 + hd * 256: 1024 + (hd + 1) * 256]
            gwin[j, hd, :, :, 512:768] = w[:, :, 2048 + hd * 256: 2048 + (hd + 1) * 256]
            gwin[j, hd, :, :, 768:800] = w[:, :, 3072:3104]
            for z in range(2):
                wg2[j, hd, z * 16:(z + 1) * 16, z * 128:(z + 1) * 128] = gla_w_g2[j, z][:, hd * 128:(hd + 1) * 128]
                wg2[j, hd, 32, z * 128:(z + 1) * 128] = gla_b_g[j, z][hd * 128:(hd + 1) * 128]
    gwin = gwin.reshape(8, 128, 6400)
    wg2f = wg2.reshape(8, 33, 256)
    gng = np.concatenate([_vec(gla_norm_g[j]) for j in range(2)], axis=1)

    in_maps = []
    for i in range(8):
        b, q = i // 4, i % 4
        X = np.concatenate([x_sample[b, q * 512:(q + 1) * 512], x_prompt[2 * i], x_prompt[2 * i + 1]], axis=0)
        cf32, cbf, posL = _consts(q)
        sel = np.zeros((128, 14), np.float32)
        sel[:, 12 + b] = 1.0
        sel[:, q] = 1.0
        if q > 0:
            sel[:, 4 + q - 1] = 1.0
        if q < 3:
            sel[:, 8 + q + 1] = 1.0
        in_maps.append(dict(
            xT=_fm(X), cT=cT, wmodsh=np.ascontiguousarray(wmod[:, 12 * q:12 * q + 12].reshape(48, 128, 1024)),
            bmodsh=np.ascontiguousarray(np.repeat(bmodv[:, :, 12 * q:12 * q + 12].reshape(128, 48), 3, axis=1)), g2=np.ascontiguousarray(g2),
            wup=wup, wdn=wdn, cw=cw, fwin=fwin, fwo=fwo, gwo=gwo, gwin=gwin,
            gwins=np.ascontiguousarray(gwin.reshape(2, 4, 128, 6400)[:, q]),
            wg2=wg2f, wg2s=np.ascontiguousarray(wg2[:, q]),
            gng=np.ascontiguousarray(gng), st0=np.ascontiguousarray(state_gla[b, :, :, q].reshape(4, 128, 256)),
            cf32=cf32, cbf=cbf, posL=posL, sel=sel))
    if 'nc' not in _NC_CACHE:
        _NC_CACHE['nc'] = build_program()
    res = run_bass_kernel_spmd(_NC_CACHE['nc'], in_maps, core_ids=list(range(8)))
    y_prompt = np.zeros((16, 256, 1024), np.float32)
    y_sample = np.zeros((2, 2048, 1024), np.float32)
    new_state = np.zeros((16, 2, 2, 4, 128, 256), np.float32)
    for i in range(8):
        b, q = i // 4, i % 4
        r = res.results[i]
        Y = np.asarray(r["yT"], dtype=np.float32).reshape(128, 8, NT).transpose(2, 1, 0).reshape(NT, 1024)
        y_sample[b, q * 512:(q + 1) * 512] = Y[0:512]
        y_prompt[2 * i] = Y[512:768]
        y_prompt[2 * i + 1] = Y[768:1024]
        ns = np.asarray(r["ns"], dtype=np.float32).reshape(2, 2, 2, 4, 128, 256)
        new_state[2 * i] = ns[0]
        new_state[2 * i + 1] = ns[1]
    return (y_prompt, y_sample, new_state)
```
